# Optimizing a Trainium2 kernel written in Bass

```python
import math
import jax, jax.numpy as jnp
from jax import lax
import numpy as np

D_MODEL = 1024
BATCH = 8
SEQ = 2048
DEPTH = 4

GRID_W = 64
CTX_LEN = 256

ATT_HEADS = 8
QK_DIM = 64
V_DIM = 2 * QK_DIM
Q_W = ATT_HEADS * 2 * QK_DIM
ATT_W = ATT_HEADS * V_DIM
Q_BLOCK = 128
ROPE_FREQS = QK_DIM // 4
ROPE_BASE = 10000.0
FOURIER_GROUPS = 4
FOURIER_GC = 128
FOURIER_W = FOURIER_GROUPS * FOURIER_GC
POOL_WINDOWS = (2, 4, 8, 16)
POOL_GROUPS = len(POOL_WINDOWS)
POOL_GC = 128
POOL_W = POOL_GROUPS * POOL_GC
N_BRANCH = 3
GATE_W = N_BRANCH * D_MODEL
OFF_K = 0
OFF_V = OFF_K + Q_W
OFF_Q = OFF_V + ATT_W
OFF_F = OFF_Q + Q_W
OFF_P = OFF_F + FOURIER_W
OFF_G = OFF_P + POOL_W
N_IN = OFF_G + GATE_W
D_FF = -(-8 * D_MODEL // (3 * 256)) * 256
ALPHA = (2 * DEPTH) ** 0.25
BETA = (8 * DEPTH) ** -0.25
LN_EPS = 1e-5

kernel_name = "hybrid_diffattn_fourier_pool_dit"


def layer_norm(x, g=None, b=None):
    xf = x.astype(jnp.float32)
    mu = jnp.mean(xf, axis=-1, keepdims=True)
    var = jnp.mean(jnp.square(xf - mu), axis=-1, keepdims=True)
    y = (xf - mu) * lax.rsqrt(var + LN_EPS)
    if g is not None:
        y = y * g.astype(jnp.float32) + b.astype(jnp.float32)
    return y.astype(x.dtype)


def modulate(h, shift, scale):
    return h * (1 + scale) + shift


def axial_rope_tables(rows, dtype):
    row = jnp.repeat(jnp.arange(rows), GRID_W).astype(jnp.float32)
    col = jnp.tile(jnp.arange(GRID_W), rows).astype(jnp.float32)
    inv = ROPE_BASE ** (-jnp.arange(ROPE_FREQS, dtype=jnp.float32) / ROPE_FREQS)
    ar = row[:, None] * inv[None, :]
    ac = col[:, None] * inv[None, :]
    ang = jnp.concatenate([ar, ar, ac, ac], axis=-1)
    return jnp.cos(ang).astype(dtype), jnp.sin(ang).astype(dtype)


def apply_rope(x, cos, sin):
    xr = x.reshape(x.shape[:-1] + (2, 2, ROPE_FREQS))
    rot = jnp.stack([-xr[..., 1, :], xr[..., 0, :]], axis=-2).reshape(x.shape)
    return x * cos[:, None, :] + rot * sin[:, None, :]


def diff_attend(q1, q2, k1, k2, v, lam):
    scale = QK_DIM ** -0.5
    s1 = jnp.einsum('bqhd,bkhd->bhqk', q1, k1).astype(jnp.float32) * scale
    s2 = jnp.einsum('bqhd,bkhd->bhqk', q2, k2).astype(jnp.float32) * scale
    a = jax.nn.softmax(s1, axis=-1) - lam * jax.nn.softmax(s2, axis=-1)
    return jnp.einsum('bhqk,bkhv->bqhv', a.astype(v.dtype), v)


def blocked_diff_attention(q1, q2, k1, k2, v, lam):
    B, S, H, _ = q1.shape
    nb = S // Q_BLOCK

    def blocks(t):
        return jnp.moveaxis(t.reshape(B, nb, Q_BLOCK, H, t.shape[-1]), 1, 0)

    out = lax.map(lambda qs: diff_attend(qs[0], qs[1], k1, k2, v, lam), (blocks(q1), blocks(q2)))
    return jnp.moveaxis(out, 0, 1).reshape(B, S, H, v.shape[-1])


def diff_head_norm(o, g, lam_init):
    B, L = o.shape[:2]
    of = o.astype(jnp.float32)
    y = of * lax.rsqrt(jnp.mean(jnp.square(of), axis=-1, keepdims=True) + LN_EPS)
    y = y * g.astype(jnp.float32) * (1.0 - lam_init)
    return y.reshape(B, L, ATT_W).astype(o.dtype)


def fourier_mix(h):
    B, L, _ = h.shape
    hg = h.reshape(B, L, FOURIER_GROUPS, FOURIER_GC).astype(jnp.float32)
    y = jnp.fft.fft2(hg, axes=(1, 3), norm='ortho').real
    return y.reshape(B, L, FOURIER_W).astype(h.dtype)


def multiscale_pool(h, w_grp, scale):
    B, L, _ = h.shape
    hg = h.reshape(B, L, POOL_GROUPS, POOL_GC).astype(jnp.float32)
    cs = jnp.concatenate([jnp.zeros_like(hg[:, :1]), jnp.cumsum(hg, axis=1)], axis=1)
    t = jnp.arange(L)
    means = []
    for g, w in enumerate(POOL_WINDOWS):
        lo = w // 2
        hi = w - lo
        start = jnp.clip(t - lo, 0, L)
        end = jnp.clip(t + hi, 0, L)
        cnt = (end - start).astype(jnp.float32)
        means.append((cs[:, end, g] - cs[:, start, g]) / cnt[None, :, None])
    pooled = jnp.stack(means, axis=2)
    y = jnp.einsum('blgc,gcd->blgd', (pooled - hg).astype(h.dtype), w_grp)
    return y.reshape(B, L, POOL_W) * scale


def mixer_output(att, tail, lam_init, subln_g_l, w_att_br_l, w_four_br_l,
                 w_pool_grp_l, pool_scale_l, w_pool_br_l, w_out_l):
    B, L = att.shape[:2]
    f_in = tail[..., :FOURIER_W]
    p_in = tail[..., FOURIER_W:FOURIER_W + POOL_W]
    gates = tail[..., FOURIER_W + POOL_W:]
    b_att = diff_head_norm(att, subln_g_l, lam_init) @ w_att_br_l
    b_four = fourier_mix(f_in) @ w_four_br_l
    b_pool = multiscale_pool(p_in, w_pool_grp_l, pool_scale_l) @ w_pool_br_l
    g = jax.nn.sigmoid(gates.astype(jnp.float32)).astype(att.dtype).reshape(B, L, N_BRANCH, D_MODEL)
    m = g[..., 0, :] * b_att + g[..., 1, :] * b_four + g[..., 2, :] * b_pool
    return m @ w_out_l


def swiglu(h, wg, wu, wd):
    return (jax.nn.silu(h @ wg) * (h @ wu)) @ wd


def setup_inputs(seed: int = 0) -> dict:
    key = jax.random.key(seed)
    ks = jax.random.split(key, 24)

    def nrm(k, shape, s):
        return jax.random.normal(k, shape, jnp.float32) * s

    D = D_MODEL
    return {
        "x": nrm(ks[0], (BATCH, SEQ, D), 1.0),
        "c": nrm(ks[1], (BATCH, D), 1.0),
        "ctx": nrm(ks[2], (BATCH, CTX_LEN, D), 1.0),
        "c_ctx": nrm(ks[3], (D,), 1.0),
        "w_mod": nrm(ks[4], (DEPTH, D, 6 * D), 0.5 * D ** -0.5),
        "b_mod": nrm(ks[5], (DEPTH, 6 * D), 0.01),
        "w_in": nrm(ks[6], (DEPTH, D, N_IN), D ** -0.5),
        "lam_qk": nrm(ks[7], (DEPTH, 4, QK_DIM), 0.1),
        "subln_g": 1.0 + nrm(ks[8], (DEPTH, V_DIM), 0.02),
        "w_att_br": nrm(ks[9], (DEPTH, ATT_W, D), ATT_W ** -0.5),
        "w_four_br": nrm(ks[10], (DEPTH, FOURIER_W, D), FOURIER_W ** -0.5),
        "w_pool_grp": nrm(ks[11], (DEPTH, POOL_GROUPS, POOL_GC, POOL_GC), POOL_GC ** -0.5),
        "pool_scale": 1.0 + nrm(ks[12], (DEPTH, POOL_W), 0.02),
        "w_pool_br": nrm(ks[13], (DEPTH, POOL_W, D), POOL_W ** -0.5),
        "w_out": nrm(ks[14], (DEPTH, D, D), BETA * D ** -0.5),
        "ln1_g": 1.0 + nrm(ks[15], (DEPTH, D), 0.02),
        "ln1_b": nrm(ks[16], (DEPTH, D), 0.01),
        "w_ffn_gate": nrm(ks[17], (DEPTH, D, D_FF), D ** -0.5),
        "w_ffn_up": nrm(ks[18], (DEPTH, D, D_FF), D ** -0.5),
        "w_ffn_down": nrm(ks[19], (DEPTH, D_FF, D), BETA * D_FF ** -0.5),
        "ln2_g": 1.0 + nrm(ks[20], (DEPTH, D), 0.02),
        "ln2_b": nrm(ks[21], (DEPTH, D), 0.01),
    }


def reference(x, c, ctx, c_ctx, w_mod, b_mod, w_in, lam_qk, subln_g, w_att_br, w_four_br,
              w_pool_grp, pool_scale, w_pool_br, w_out, ln1_g, ln1_b, w_ffn_gate, w_ffn_up,
              w_ffn_down, ln2_g, ln2_b):
    B, S, _ = x.shape
    Lc = ctx.shape[1]
    H = ATT_HEADS
    ROWS = S // GRID_W
    cos, sin = axial_rope_tables(ROWS, x.dtype)
    xc = ctx
    silu_c = jax.nn.silu(c)
    silu_cc = jax.nn.silu(c_ctx)

    for l in range(DEPTH):
        last = l == DEPTH - 1
        lam_init = 0.8 - 0.6 * math.exp(-0.3 * l)
        lf = lam_qk[l].astype(jnp.float32)
        lam = jnp.exp(jnp.sum(lf[0] * lf[1])) - jnp.exp(jnp.sum(lf[2] * lf[3])) + lam_init

        mod = (silu_c @ w_mod[l] + b_mod[l])[:, None, :]
        modc = (silu_cc @ w_mod[l] + b_mod[l])[None, None, :]
        sh1, sc1, g1, sh2, sc2, g2 = jnp.split(mod, 6, axis=-1)
        csh1, csc1, cg1, csh2, csc2, cg2 = jnp.split(modc, 6, axis=-1)

        uc = modulate(layer_norm(xc), csh1, csc1)
        pc_kv = uc @ w_in[l][:, :OFF_Q]
        ck = pc_kv[..., OFF_K:OFF_V].reshape(B, Lc, H, 2, QK_DIM)
        cv = pc_kv[..., OFF_V:OFF_Q].reshape(B, Lc, H, V_DIM)

        u = modulate(layer_norm(x), sh1, sc1)
        p = u @ w_in[l]
        k = p[..., OFF_K:OFF_V].reshape(B, S, H, 2, QK_DIM)
        v = p[..., OFF_V:OFF_Q].reshape(B, S, H, V_DIM)
        q = p[..., OFF_Q:OFF_F].reshape(B, S, H, 2, QK_DIM)
        q1 = apply_rope(q[..., 0, :], cos, sin)
        q2 = apply_rope(q[..., 1, :], cos, sin)
        k1 = jnp.concatenate([apply_rope(k[..., 0, :], cos, sin), ck[..., 0, :]], axis=1)
        k2 = jnp.concatenate([apply_rope(k[..., 1, :], cos, sin), ck[..., 1, :]], axis=1)
        vv = jnp.concatenate([v, cv], axis=1)
        att = blocked_diff_attention(q1, q2, k1, k2, vv, lam)
        y = mixer_output(att, p[..., OFF_F:], lam_init, subln_g[l], w_att_br[l], w_four_br[l],
                         w_pool_grp[l], pool_scale[l], w_pool_br[l], w_out[l])
        x = layer_norm(ALPHA * x + g1 * y, ln1_g[l], ln1_b[l])

        if not last:
            pc = uc @ w_in[l][:, OFF_Q:]
            cq = pc[..., :Q_W].reshape(B, Lc, H, 2, QK_DIM)
            catt = diff_attend(cq[..., 0, :], cq[..., 1, :], ck[..., 0, :], ck[..., 1, :], cv, lam)
            yc = mixer_output(catt, pc[..., Q_W:], lam_init, subln_g[l], w_att_br[l], w_four_br[l],
                              w_pool_grp[l], pool_scale[l], w_pool_br[l], w_out[l])
            xc = layer_norm(ALPHA * xc + cg1 * yc, ln1_g[l], ln1_b[l])

        f = swiglu(modulate(layer_norm(x), sh2, sc2), w_ffn_gate[l], w_ffn_up[l], w_ffn_down[l])
        x = layer_norm(ALPHA * x + g2 * f, ln2_g[l], ln2_b[l])
        if not last:
            fc = swiglu(modulate(layer_norm(xc), csh2, csc2), w_ffn_gate[l], w_ffn_up[l], w_ffn_down[l])
            xc = layer_norm(ALPHA * xc + cg2 * fc, ln2_g[l], ln2_b[l])

    return x
```

```python
import math
from contextlib import ExitStack

import ml_dtypes
import numpy as np

import concourse.bass as bass
import concourse.mybir as mybir
from concourse.bass_utils import run_bass_kernel_spmd

F32 = mybir.dt.float32
BF16 = mybir.dt.bfloat16
AF = mybir.ActivationFunctionType
ALU = mybir.AluOpType
AX = mybir.AxisListType

D = 1024
S = 2048
LC = 256
NT = S + LC
NB = NT // 128
DEPTH = 4
H = 8
N_IN = 7168
OFF_K, OFF_V, OFF_Q, OFF_F, OFF_P, OFF_G = 0, 1024, 2048, 3072, 3584, 4096
DFF = 2816
NFF = DFF // 128
ALPHA = (2 * DEPTH) ** 0.25
EPS = 1e-5
GRID_W = 64

ENGS = ("pe", "act", "dve", "pool", "sp")


class Prog:
    def __init__(self):
        self.ops = {e: [] for e in ENGS}
        self.lastw = {}
        self.readers = {}
        self.dma_cnt = {}
        self.dma_last = {}
        self.sig = {e: set() for e in ENGS}

    @staticmethod
    def _merge(deps, tok):
        k = (tok[0], tok[1])
        if deps.get(k, -1) < tok[2]:
            deps[k] = tok[2]

    def op(self, eng, fn, reads=(), writes=(), dma=None, ndma=1):
        idx = len(self.ops[eng])
        deps = {}
        for r in reads:
            w = self.lastw.get(r)
            if w is not None:
                self._merge(deps, w)
            if isinstance(r, tuple) and r[0] == "ps":
                for k, v in self.readers.get(r, {}).items():
                    if k[1] != eng:
                        self._merge(deps, (k[0], k[1], v))
        for r in writes:
            w = self.lastw.get(r)
            if w is not None:
                self._merge(deps, w)
            for k, v in self.readers.get(r, {}).items():
                self._merge(deps, (k[0], k[1], v))
        rec = dict(fn=fn, deps=deps, dma=dma, ndma=ndma)
        if dma is not None:
            prev = self.dma_last.get(dma)
            if prev is not None:
                self._merge(deps, prev)
            self.dma_cnt[dma] = self.dma_cnt.get(dma, 0) + ndma
            tok = ("d", dma, 16 * self.dma_cnt[dma])
            self.dma_last[dma] = tok
        else:
            tok = ("c", eng, idx)
        deps.pop(("c", eng, idx), None)
        for k in deps:
            if k[0] == "c":
                self.sig[k[1]].add(deps[k])
        self.ops[eng].append(rec)
        for r in reads:
            rd = self.readers.setdefault(r, {})
            k = (tok[0], tok[1])
            if rd.get(k, -1) < tok[2]:
                rd[k] = tok[2]
        for r in writes:
            self.lastw[r] = tok
            self.readers[r] = {}
        return tok

    def fence(self):
        deps = {}
        for e in ENGS:
            real = [i for i, r in enumerate(self.ops[e]) if r["fn"] is not None and r["dma"] is None]
            if real:
                deps[("c", e)] = real[-1]
        for k, tok in self.dma_last.items():
            deps[("d", k)] = tok[2]
        for e in ENGS:
            d = dict(deps)
            idx = len(self.ops[e])
            d.pop(("c", e), None) if False else None
            for k in d:
                if k[0] == "c":
                    self.sig[k[1]].add(d[k])
            self.ops[e].append(dict(fn=None, deps=d, dma=None, ndma=0))
        self.lastw = {}
        self.readers = {}

    def emit(self, nc, es):
        esem = {e: es.enter_context(nc.semaphore("s_" + e)) for e in ENGS}
        dsem = {k: es.enter_context(nc.semaphore("d_%d" % i)) for i, k in enumerate(self.dma_cnt)}
        sigcount = {}
        for e in ENGS:
            c = 0
            m = {}
            for i in range(len(self.ops[e])):
                if i in self.sig[e]:
                    c += 1
                    m[i] = c
            sigcount[e] = m
        block = es.enter_context(nc.Block())
        stats = {}

        def make(e):
            def body(eng):
                waited = {}
                nw = 0
                for i, rec in enumerate(self.ops[e]):
                    for k, v in rec["deps"].items():
                        if k[0] == "c":
                            if k[1] == e and e == "pe":
                                continue
                            if k[1] == e and v >= i:
                                continue
                            sem = esem[k[1]]
                            val = sigcount[k[1]][v]
                            wk = ("c", k[1])
                        else:
                            sem = dsem[k[1]]
                            val = v
                            wk = ("d", k[1])
                        if waited.get(wk, 0) >= val:
                            continue
                        waited[wk] = val
                        eng.wait_ge(sem, val)
                        nw += 1
                    if rec["fn"] is None:
                        continue
                    ins = rec["fn"](eng)
                    if rec["dma"] is not None:
                        if not isinstance(ins, (list, tuple)):
                            ins = [ins]
                        assert len(ins) == rec["ndma"], (len(ins), rec["ndma"])
                        for x in ins:
                            x.then_inc(dsem[rec["dma"]], 16)
                    elif i in self.sig[e]:
                        if isinstance(ins, (list, tuple)):
                            ins = ins[-1]
                        ins.then_inc(esem[e], 1)
                stats[e] = (len(self.ops[e]), nw)
            return body

        block.tensor(make("pe"))
        block.scalar(make("act"))
        block.vector(make("dve"))
        block.gpsimd(make("pool"))
        block.sync(make("sp"))
        return stats


def _bf(a):
    return np.ascontiguousarray(np.asarray(a, np.float32)).astype(ml_dtypes.bfloat16)


def make_consts():
    c = {}
    c["ident"] = _bf(np.eye(128))
    t = np.arange(S)
    row = (t // GRID_W).astype(np.float64)
    col = (t % GRID_W).astype(np.float64)
    inv = 10000.0 ** (-np.arange(16, dtype=np.float64) / 16)
    ang = np.zeros((64, S))
    for d in range(64):
        axis, f = d // 32, d % 16
        ang[d] = (row if axis == 0 else col) * inv[f]
    cos = np.cos(ang)
    sin = np.sin(ang)
    c["rope_cos"] = _bf(np.concatenate([cos, cos], 0))
    c["rope_sin"] = _bf(np.concatenate([sin, sin], 0))
    pr = np.zeros((128, 128))
    for m in range(2):
        for d in range(64):
            half = (d % 32) // 16
            if half == 0:
                pr[m * 64 + d + 16, m * 64 + d] = -1.0
            else:
                pr[m * 64 + d - 16, m * 64 + d] = 1.0
    c["prot"] = _bf(pr)
    k = np.arange(128)
    a = 2 * np.pi * np.outer(k, k) / 128
    c["ccsc"] = _bf(np.concatenate([np.cos(a), np.sin(a)], 1))
    l = np.arange(S)
    a = 2 * np.pi * ((np.outer(l, l)) % S) / S
    c["dft"] = _bf(np.stack([np.cos(a), -np.sin(a)], 0))
    l = np.arange(LC)
    a = 2 * np.pi * ((np.outer(l, l)) % LC) / LC
    c["dft256"] = _bf(np.stack([np.cos(a), -np.sin(a)], 0))
    L = 384
    bands = np.zeros((128, 4, 5, 128))
    for g, w in enumerate((2, 4, 8, 16)):
        lo = w // 2
        hi = w - lo
        M = np.zeros((L, L))
        for tp in range(L):
            s0 = min(max(tp - lo, 0), L)
            e0 = min(max(tp + hi, 0), L)
            M[s0:e0, tp] = 1.0 / (e0 - s0)
            M[tp, tp] -= 1.0
        bands[:, g, 0] = M[0:128, 128:256]
        bands[:, g, 1] = M[128:256, 128:256]
        bands[:, g, 2] = M[256:384, 128:256]
        bands[:, g, 3] = M[0:128, 0:128]
        bands[:, g, 4] = M[256:384, 256:384]
    c["bands"] = _bf(bands.reshape(128, 20, 128))
    return c


TCH = [(0, 512), (512, 512), (1024, 512), (1536, 512), (2048, 256)]
ARENA_BYTES = 70 * 1024


def build_program(n_layers=DEPTH, dbg=None):
    dbg = dbg or {}
    nc = bass.Bass("TRN2", target_bir_lowering=False)
    P = Prog()
    es = ExitStack()

    declared = []

    def dr_now(name, shape, dt, kind="ExternalInput"):
        declared.append(name)
        return nc.dram_tensor(name, list(shape), dt, kind=kind).ap()

    class Lazy:
        def __init__(self, name, shape, dt, kind):
            self.args = (name, shape, dt, kind)
            self._ap = None

        def ap(self):
            if self._ap is None:
                self._ap = dr_now(*self.args)
            return self._ap

        def __getitem__(self, k):
            return self.ap()[k]

        def rearrange(self, *a, **kw):
            return self.ap().rearrange(*a, **kw)

    def dr(name, shape, dt, kind="ExternalInput"):
        if kind == "ExternalInput":
            return Lazy(name, shape, dt, kind)
        return dr_now(name, shape, dt, kind)

    x_d = dr("x", [S, D], F32)
    ctx_d = dr("ctx", [LC, D], F32)
    cT_d = dr("cT", [128, 16], F32)
    w_mod = dr("w_mod", [DEPTH, D, 6 * D], F32)
    b_mod = dr("b_mod", [DEPTH, 6 * D], F32)
    b_modT = dr("b_modT", [DEPTH, 128, 48], F32)
    w_in = dr("w_in", [DEPTH, D, N_IN], F32)
    lam_qk = dr("lam_qk", [DEPTH, 256], F32)
    subln_gT = dr("subln_gT", [DEPTH, 128, 1], F32)
    w_att_br = dr("w_att_br", [DEPTH, D, D], F32)
    w_four_br = dr("w_four_br", [DEPTH, 512, D], F32)
    w_pool_grp = dr("w_pool_grp", [DEPTH, 4, 128, 128], F32)
    pool_scaleT = dr("pool_scaleT", [DEPTH, 128, 4], F32)
    w_pool_br = dr("w_pool_br", [DEPTH, 512, D], F32)
    w_out = dr("w_out", [DEPTH, D, D], F32)
    ln1_g = dr("ln1_g", [DEPTH, D], F32)
    ln1_b = dr("ln1_b", [DEPTH, D], F32)
    w_ffn_gate = dr("w_ffn_gate", [DEPTH, D, DFF], F32)
    w_ffn_up = dr("w_ffn_up", [DEPTH, D, DFF], F32)
    w_ffn_down = dr("w_ffn_down", [DEPTH, DFF, D], F32)
    ln2_g = dr("ln2_g", [DEPTH, D], F32)
    ln2_b = dr("ln2_b", [DEPTH, D], F32)
    ident_d = dr("ident", [128, 128], BF16)
    cos_d = dr("rope_cos", [128, S], BF16)
    sin_d = dr("rope_sin", [128, S], BF16)
    prot_d = dr("prot", [128, 128], BF16)
    ccsc_d = dr("ccsc", [128, 256], BF16)
    dft_d = dr("dft", [2, S, S], BF16)
    dft256_d = dr("dft256", [2, LC, LC], BF16)
    bands_d = dr("bands", [128, 20, 128], BF16)
    out_d = dr("out", [S, D], F32, "ExternalOutput")
    skind = "ExternalOutput" if dbg.get("scr_out") else "Internal"
    att_d = dr("att_scr", [H, 128, NT], BF16, skind)
    y_d = dr("y_scr", [4, 128, NT], BF16, skind)
    pool_d = dr("pool_scr", [4, 128, NT], BF16, skind)
    wgu_scr = dr("wgu_scr", [NFF // 2, 128, 4096], BF16, "Internal")
    wd_scr = dr("wd_scr", [NFF // 2, 128, 2 * D], BF16, "Internal")
    wg3_scr = dr("wg3_scr", [8, 128, 3072], BF16, "Internal")
    wab_scr = dr("wab_scr", [8, 128, 1024], BF16, "Internal")
    wfb_scr = dr("wfb_scr", [8, 128, 512], BF16, "Internal")
    wpb_scr = dr("wpb_scr", [8, 128, 512], BF16, "Internal")
    dbg_out = {}

    def sb(name, shape, dt):
        return es.enter_context(nc.sbuf_tensor(name, list(shape), dt))

    X = sb("X", [128, NB, D], F32)
    UT = sb("UT", [128, 8, NT], BF16)
    IDN = sb("IDN", [128, 128], BF16)
    ONES = sb("ONES", [128, 128], BF16)
    ONES32 = sb("ONES32", [128, 128], F32)
    GBC = sb("GBC", [128, 4, D], F32)
    LNBC = sb("LNBC", [128, 2, D], F32)
    MODT = sb("MODT", [128, 4, 8, 2], F32)
    BMT = sb("BMT", [128, 48], F32)
    CT = sb("CT", [128, 16], F32)
    SC = sb("SC", [128, 16], F32)
    S2 = sb("S2", [128, 8, 2], BF16)
    ST = sb("ST", [128, NB, 2, 6], F32)
    MV = sb("MV", [128, NB, 2], F32)
    LNV = sb("LNV", [128, NB], F32)
    RS = sb("RS", [128, NB], F32)
    NBI = sb("NBI", [128, NB], F32)
    LQ = sb("LQ", [128, 256], F32)
    LQP = sb("LQP", [128, 2, 64], F32)
    LQS = sb("LQS", [128, 2], F32)
    LQE = sb("LQE", [128, 2], F32)
    NLAM = sb("NLAM", [128, 1], F32)
    SGS = sb("SGS", [128, 1], F32)
    PSC = sb("PSC", [128, 4], F32)
    ARENA = sb("ARENA", [128, ARENA_BYTES // 2], BF16)
    PS = es.enter_context(nc.psum_tensor("PS", [128, 8, 512], F32))

    class Arena:
        def __init__(self):
            self.off = 0

        def reset(self):
            self.off = 0

        def alloc(self, shape, dt):
            n = 1
            for s_ in shape[1:]:
                n *= s_
            nbytes = n * (4 if dt == F32 else 2)
            nbytes = (nbytes + 63) // 64 * 64
            assert self.off + nbytes <= ARENA_BYTES, ("arena overflow", self.off, nbytes)
            ap = ARENA[:, self.off // 2:(self.off + nbytes) // 2]
            self.off += nbytes
            if dt == F32:
                ap = ap.bitcast(F32)
            ap = ap[:, 0:n]
            if len(shape) == 2:
                return ap
            names = " ".join("d%d" % i for i in range(1, len(shape)))
            kw = {"d%d" % i: shape[i] for i in range(1, len(shape))}
            return ap.rearrange("p (%s) -> p %s" % (names, names), **kw)

    A = Arena()
    bank_ctr = [0]

    def nbank():
        b = bank_ctr[0] % 8
        bank_ctr[0] += 1
        return b

    def psb(b):
        return ("ps", b)

    def mm(out, pairs, reads, writes):
        pairs = list(pairs)

        def fn(e):
            n = len(pairs)
            ins = None
            for i, (l_, r_) in enumerate(pairs):
                ins = e.matmul(out, l_, r_, start=(i == 0), stop=(i == n - 1))
            return ins
        P.op("pe", fn, reads=reads, writes=writes)

    def ut_keys(t0, T):
        return [("UT", tb, kc) for tb in range(t0 // 128, (t0 + T) // 128) for kc in range(8)]

    def dma_cast(key, out, in_, reads=(), writes=(), n=1):
        P.op("pool", lambda e: e.dma_start(out=out, in_=in_), reads=reads, writes=writes, dma=key)

    def dma_sp(key, out, in_, reads=(), writes=()):
        P.op("sp", lambda e: e.dma_start(out=out, in_=in_), reads=reads, writes=writes, dma=key)

    def wview(w2d, c0, ncols):
        return w2d[:, c0:c0 + ncols].rearrange("(kc p) n -> p kc n", p=128)

    xv = x_d.rearrange("(tb p) d -> p tb d", p=128)
    for i in range(4):
        dma_sp("ldx%d" % i, X[:, 4 * i:4 * i + 4, :], xv[:, 4 * i:4 * i + 4, :],
               writes=[("X", t) for t in range(4 * i, 4 * i + 4)])
    dma_sp("ldx4", X[:, 16:18, :], ctx_d.rearrange("(tb p) d -> p tb d", p=128), writes=[("X", 16), ("X", 17)])
    dma_sp("ldc", IDN[:], ident_d[:, :], writes=["IDN"])
    dma_sp("ldc", CT[:], cT_d[:, :], writes=["CT"])
    P.op("dve", lambda e: e.memset(ONES[:], 1.0), writes=["ONES"])
    P.op("dve", lambda e: e.memset(ONES32[:], 1.0), writes=["ONES32"])
    P.op("act", lambda e: e.activation(out=SC[:], in_=CT[:], func=AF.Silu), reads=["CT"], writes=["SC"])
    P.op("dve", lambda e: e.tensor_copy(S2[:, :, 0], SC[:, 0:8]), reads=["SC"], writes=["S2"])
    P.op("dve", lambda e: e.tensor_copy(S2[:, :, 1], SC[:, 8:16]), reads=["SC"], writes=["S2"])

    def emit_mod(l):
        A.reset()
        SREP = A.alloc([128, 2, 8, 128], BF16)
        for lc in range(2):
            P.op("dve", lambda e, lc=lc: e.tensor_copy(SREP[:, lc], SC[:, 8 * lc:8 * lc + 8].unsqueeze(2).to_broadcast([128, 8, 128])),
                 reads=["SC"], writes=["SREP"])
        WM = [A.alloc([128, 8, 512], BF16) for _ in range(2)]
        BR = [A.alloc([128, 512], F32) for _ in range(2)]
        dma_sp("ldm", BMT[:], b_modT[l], writes=["BMT"])
        for j in range(12):
            sec = j // 2
            slot = j % 2
            dma_cast("wm%d" % slot, WM[slot], wview(w_mod[l], j * 512, 512), writes=[("WM", slot)])
            if sec in (2, 5):
                gi = 0 if sec == 2 else 1
                dma_sp("br%d" % slot, BR[slot], b_mod[l, j * 512:(j + 1) * 512].partition_broadcast(128),
                       writes=[("BR", slot)])
                for lc in range(2):
                    b = nbank()
                    mm(PS[:, b, :], [(SREP[:, lc, kc, :], WM[slot][:, kc, :]) for kc in range(8)],
                       reads=["SREP", ("WM", slot)], writes=[psb(b)])
                    P.op("dve", lambda e, b=b, lc=lc, gi=gi, slot=slot, j=j: e.tensor_tensor(
                        GBC[:, 2 * lc + gi, (j % 2) * 512:(j % 2) * 512 + 512], PS[:, b, :], BR[slot], ALU.add),
                        reads=[psb(b), ("BR", slot)], writes=[("GBC", 2 * lc + gi, j % 2)])
            else:
                s_ = {0: 0, 1: 1, 3: 2, 4: 3}[sec]
                b = nbank()
                for f in range(4):
                    mm(PS[:, b, 2 * f:2 * f + 2], [(WM[slot][:, kc, f * 128:(f + 1) * 128], S2[:, kc, :]) for kc in range(8)],
                       reads=["S2", ("WM", slot)], writes=[psb(b)])
                h0 = (j % 2) * 4
                P.op("dve", lambda e, b=b, s_=s_, h0=h0, j=j: e.tensor_tensor(
                    MODT[:, s_, h0:h0 + 4, :], PS[:, b, 0:8].rearrange("p (f c) -> p f c", c=2),
                    BMT[:, j * 4:j * 4 + 4].unsqueeze(2).to_broadcast([128, 4, 2]), ALU.add),
                    reads=[psb(b), "BMT"], writes=[("MODT", s_, j % 2)])
                if s_ in (1, 3):
                    P.op("dve", lambda e, s_=s_, h0=h0: e.tensor_scalar(
                        MODT[:, s_, h0:h0 + 4, :], MODT[:, s_, h0:h0 + 4, :], 1.0, None, ALU.add),
                        reads=[("MODT", s_, j % 2)], writes=[("MODT", s_, j % 2)])
        lam_init = 0.8 - 0.6 * math.exp(-0.3 * l)
        dma_sp("ldm", LQ[:], lam_qk[l].partition_broadcast(128), writes=["LQ"])
        LQv = LQ[:].rearrange("p (a b d) -> p a b d", a=2, b=2)
        P.op("dve", lambda e: e.tensor_tensor(LQP[:], LQv[:, :, 0, :], LQv[:, :, 1, :], ALU.mult), reads=["LQ"], writes=["LQP"])
        P.op("dve", lambda e: e.reduce_sum(LQS[:], LQP[:], AX.X), reads=["LQP"], writes=["LQS"])
        P.op("act", lambda e: e.activation(out=LQE[:], in_=LQS[:], func=AF.Exp), reads=["LQS"], writes=["LQE"])
        P.op("dve", lambda e: e.tensor_tensor(NLAM[:], LQE[:, 1:2], LQE[:, 0:1], ALU.subtract), reads=["LQE"], writes=["NLAM"])
        P.op("dve", lambda e: e.tensor_scalar(NLAM[:], NLAM[:], -lam_init, None, ALU.add), reads=["NLAM"], writes=["NLAM"])
        dma_sp("ldm", SGS[:], subln_gT[l], writes=["SGS"])
        P.op("dve", lambda e: e.tensor_scalar(SGS[:], SGS[:], (1.0 - lam_init) * math.sqrt(128.0), None, ALU.mult),
             reads=["SGS"], writes=["SGS"])
        dma_sp("ldm", PSC[:], pool_scaleT[l], writes=["PSC"])

    def emit_ln_ut(s_sh, s_sc, nblk, reset=True):
        if reset:
            A.reset()
        XH = [A.alloc([128, D], BF16) for _ in range(3)]
        lnst = dbg.get("ln_stage", 9)

        def stats_pair(tp):
            for tb in (2 * tp, 2 * tp + 1):
                P.op("dve", lambda e, tb=tb: [e.bn_stats(ST[:, tb, 0, :], X[:, tb, 0:512]),
                                              e.bn_stats(ST[:, tb, 1, :], X[:, tb, 512:1024])],
                     reads=[("X", tb)], writes=[("ST", tb)])
                P.op("dve", lambda e, tb=tb: e.bn_aggr(MV[:, tb, :], ST[:, tb, :, :]), reads=[("ST", tb)], writes=[("MV", tb)])
            a, b_ = 2 * tp, 2 * tp + 2
            mvk = [("MV", a), ("MV", a + 1)]
            P.op("act", lambda e: e.activation(out=LNV[:, a:b_], in_=MV[:, a:b_, 1], func=AF.Ln, bias=EPS, scale=1.0),
                 reads=mvk, writes=[("LNV", tp)])
            P.op("act", lambda e: e.activation(out=RS[:, a:b_], in_=LNV[:, a:b_], func=AF.Exp, scale=-0.5),
                 reads=[("LNV", tp)], writes=[("RS", a), ("RS", a + 1)])
            P.op("dve", lambda e: e.scalar_tensor_tensor(NBI[:, a:b_], MV[:, a:b_, 0], -1.0, RS[:, a:b_], ALU.mult, ALU.mult),
                 reads=mvk + [("RS", a), ("RS", a + 1)], writes=[("NBI", a), ("NBI", a + 1)])

        stats_pair(0)
        for tp in range(nblk // 2):
            b0 = 2 * (tp % 4)
            lc = 1 if tp * 2 >= 16 else 0
            for i in range(2):
                tb = 2 * tp + i
                xh = XH[tb % 3]
                P.op("act", lambda e, tb=tb, xh=xh: e.activation(out=xh, in_=X[:, tb, :], func=AF.Identity,
                                                               scale=RS[:, tb:tb + 1], bias=NBI[:, tb:tb + 1]),
                     reads=[("X", tb), ("RS", tb), ("NBI", tb)], writes=[("XH", tb % 3)])
                pv = PS[:, b0 + i, :].bitcast(BF16)

                def tfn(e, xh=xh, pv=pv):
                    ins = None
                    for kc in range(8):
                        ins = e.transpose(pv[:, kc * 128:(kc + 1) * 128], xh[:, kc * 128:(kc + 1) * 128], IDN[:])
                    return ins
                P.op("pe", tfn, reads=[("XH", tb % 3), "IDN"], writes=[psb(b0 + i)])
            if tp + 1 < nblk // 2:
                stats_pair(tp + 1)
            pv2 = PS[:, b0:b0 + 2, :].bitcast(BF16)
            for kc in range(8):
                src = pv2[:, :, kc * 128:(kc + 1) * 128]
                dst = UT[:, kc, 2 * tp * 128:(2 * tp + 2) * 128].rearrange("p (b n) -> p b n", b=2)
                sc_ap = MODT[:, s_sc, kc, lc:lc + 1]
                sh_ap = MODT[:, s_sh, kc, lc:lc + 1]
                rd = [psb(b0), psb(b0 + 1), ("MODT", s_sc, kc // 4), ("MODT", s_sh, kc // 4)]
                wr = [("UT", 2 * tp, kc), ("UT", 2 * tp + 1, kc)]
                if tp % 2 == 0:
                    P.op("act", lambda e, dst=dst, src=src, sc_ap=sc_ap, sh_ap=sh_ap: e.activation(
                        out=dst, in_=src, func=AF.Identity, scale=sc_ap, bias=sh_ap), reads=rd, writes=wr)
                else:
                    P.op("dve", lambda e, dst=dst, src=src, sc_ap=sc_ap, sh_ap=sh_ap: e.tensor_scalar(
                        dst, src, sc_ap, sh_ap, ALU.mult, ALU.add), reads=rd, writes=wr)

    def evac_copy(i, dst, src, reads, writes, scale=None):
        if i % 2 == 0:
            if scale is None:
                P.op("act", lambda e: e.activation(out=dst, in_=src, func=AF.Copy), reads=reads, writes=writes)
            else:
                P.op("act", lambda e: e.activation(out=dst, in_=src, func=AF.Copy, scale=scale), reads=reads, writes=writes)
        else:
            if scale is None:
                P.op("dve", lambda e: e.tensor_copy(dst, src), reads=reads, writes=writes)
            else:
                P.op("dve", lambda e: e.tensor_scalar(dst, src, scale, None, ALU.mult), reads=reads, writes=writes)

    def emit_fourier(l, last):
        A.reset()
        chunks = TCH[:4] if last else TCH
        nblk = 16 if last else NB
        CCSC = A.alloc([128, 256], BF16)
        D256 = A.alloc([128, 2, 2, 256], BF16)
        WF = [A.alloc([128, 8, 256], BF16) for _ in range(2)]
        FT = [A.alloc([128, NT], BF16) for _ in range(2)]
        ABt = A.alloc([128, 2, NB, 256], BF16)
        DS = [A.alloc([128, 4, 2, 512], BF16) for _ in range(2)]
        YS = [A.alloc([128, 512], BF16) for _ in range(2)]
        dma_sp("ldc", CCSC, ccsc_d[:, :], writes=["CCSC"])
        if not last:
            for cs in range(2):
                dma_sp("ldc", D256[:, :, cs, :], dft256_d[cs].rearrange("(lb p) n -> p lb n", p=128), writes=["D256"])
        ev = 0
        for gp in range(2):
            dma_cast("wf%d" % gp, WF[gp], wview(w_in[l], OFF_F + gp * 256, 256), writes=[("WF", gp)])
            for g2 in range(2):
                for (t0, T) in chunks:
                    b = nbank()
                    mm(PS[:, b, 0:T], [(WF[gp][:, kc, g2 * 128:(g2 + 1) * 128], UT[:, kc, t0:t0 + T]) for kc in range(8)],
                       reads=[("WF", gp)] + ut_keys(t0, T), writes=[psb(b)])
                    evac_copy(ev, FT[g2][:, t0:t0 + T], PS[:, b, 0:T], [psb(b)], [("FT", g2, t0 // 512)])
                    ev += 1
            for tb in range(nblk):
                b = nbank()
                for g2 in range(2):
                    mm(PS[:, b, g2 * 256:(g2 + 1) * 256], [(FT[g2][:, tb * 128:(tb + 1) * 128], CCSC)],
                       reads=[("FT", g2, tb // 4), "CCSC"], writes=[psb(b)])
                evac_copy(ev, ABt[:, :, tb, :], PS[:, b, :].rearrange("p (g n) -> p g n", g=2), [psb(b)], [("AB", tb)])
                ev += 1
            for lq in range(4):
                bY = [nbank(), nbank()]
                for lbg in range(4):
                    slot = (lq * 4 + lbg) % 2
                    for cs, q_ in ((0, "sp"), (1, "pool")):
                        P.op(q_, lambda e, slot=slot, lbg=lbg, lq=lq, cs=cs: e.dma_start(
                            out=DS[slot][:, :, cs, :],
                            in_=dft_d[cs, lbg * 512:(lbg + 1) * 512, lq * 512:(lq + 1) * 512].rearrange("(lb p) n -> p lb n", p=128)),
                            writes=[("DS", slot, cs)], dma="ds%d_%d" % (slot, cs))

                    def fn(e, lbg=lbg, slot=slot, bY=bY):
                        ins = None
                        for lbi in range(4):
                            lb = lbg * 4 + lbi
                            for cs in range(2):
                                for g2 in range(2):
                                    ins = e.matmul(PS[:, bY[g2], :], ABt[:, g2, lb, cs * 128:(cs + 1) * 128], DS[slot][:, lbi, cs, :],
                                                   start=(lb == 0 and cs == 0), stop=(lb == 15 and cs == 1))
                        return ins
                    P.op("pe", fn, reads=[("AB", lbg * 4 + i) for i in range(4)] + [("DS", slot, 0), ("DS", slot, 1)],
                         writes=[psb(bY[0]), psb(bY[1])])
                for g2 in range(2):
                    evac_copy(ev, YS[g2], PS[:, bY[g2], :], [psb(bY[g2])], [("YS", g2)], scale=1.0 / 512.0)
                    ev += 1
                    dma_sp("ys%d" % g2, y_d[gp * 2 + g2, :, lq * 512:(lq + 1) * 512], YS[g2], reads=[("YS", g2)],
                           writes=[("Yd", gp * 2 + g2, lq)])
            if not last:
                for g2 in range(2):
                    b = nbank()
                    mm(PS[:, b, 0:256], [(ABt[:, g2, 16 + lb, cs * 128:(cs + 1) * 128], D256[:, lb, cs, :])
                                         for lb in range(2) for cs in range(2)],
                       reads=[("AB", 16), ("AB", 17), "D256"], writes=[psb(b)])
                    evac_copy(ev, YS[g2][:, 0:256], PS[:, b, 0:256], [psb(b)], [("YS", g2)], scale=1.0 / math.sqrt(256.0 * 128.0))
                    ev += 1
                    dma_sp("ys%d" % g2, y_d[gp * 2 + g2, :, S:NT], YS[g2][:, 0:256], reads=[("YS", g2)],
                           writes=[("Yd", gp * 2 + g2, 4)])

    def emit_pool(l, last):
        A.reset()
        chunks = TCH[:4] if last else TCH
        nblk = 16 if last else NB
        WP = A.alloc([128, 8, 512], BF16)
        PTOK = A.alloc([128, NB, 512], BF16)
        BND = A.alloc([128, 20, 128], BF16)
        WPG = A.alloc([128, 4, 128], BF16)
        DT = [A.alloc([128, 512], BF16) for _ in range(4)]
        PST = [A.alloc([128, 512], BF16) for _ in range(4)]
        dma_cast("wp", WP, wview(w_in[l], OFF_P, 512), writes=["WP"])
        dma_sp("ldc", BND, bands_d[:, :, :], writes=["BND"])
        dma_cast("wpg", WPG, w_pool_grp[l].rearrange("g c d -> c g d"), writes=["WPG"])
        ev = 0
        for tb in range(nblk):
            b = nbank()
            mm(PS[:, b, :], [(UT[:, kc, tb * 128:(tb + 1) * 128], WP[:, kc, :]) for kc in range(8)],
               reads=ut_keys(tb * 128, 128) + ["WP"], writes=[psb(b)])
            evac_copy(ev, PTOK[:, tb, :], PS[:, b, :], [psb(b)], [("PTOK", tb)])
            ev += 1
        for (t0, T) in chunks:
            first, lastb = (0, 15) if t0 < S else (16, 17)
            obs = list(range(t0 // 128, (t0 + T) // 128))
            rb = [("PTOK", tb) for tb in range(max(first, obs[0] - 1), min(lastb, obs[-1] + 1) + 1)]
            bb = [nbank() for _ in range(4)]
            for g in range(4):
                b = bb[g]

                def fn(e, g=g, b=b, obs=obs, first=first, lastb=lastb, t0=t0):
                    ins = None
                    for ob in obs:
                        srcs = []
                        if ob > first:
                            srcs.append((ob - 1, 0))
                        srcs.append((ob, 3 if ob == first else (4 if ob == lastb else 1)))
                        if ob < lastb:
                            srcs.append((ob + 1, 2))
                        for i, (ib, var) in enumerate(srcs):
                            ins = e.matmul(PS[:, b, ob * 128 - t0:ob * 128 - t0 + 128], PTOK[:, ib, g * 128:(g + 1) * 128],
                                           BND[:, g * 5 + var, :], start=(i == 0), stop=(i == len(srcs) - 1))
                    return ins
                P.op("pe", fn, reads=rb + ["BND"], writes=[psb(b)])
            for g in range(4):
                evac_copy(g, DT[g][:, 0:T], PS[:, bb[g], 0:T], [psb(bb[g])], [("DT", g)])
            b2 = [nbank() for _ in range(4)]
            for g in range(4):
                mm(PS[:, b2[g], 0:T], [(WPG[:, g, :], DT[g][:, 0:T])], reads=["WPG", ("DT", g)], writes=[psb(b2[g])])
            for g in range(4):
                P.op("dve", lambda e, g=g, T=T, b=b2[g]: e.tensor_scalar(PST[g][:, 0:T], PS[:, b, 0:T], PSC[:, g:g + 1], None, ALU.mult),
                     reads=[psb(b2[g]), "PSC"], writes=[("PST", g)])
                dma_sp("ps%d" % g, pool_d[g, :, t0:t0 + T], PST[g][:, 0:T], reads=[("PST", g)], writes=[("Pd", g, t0 // 512)])

    def emit_attn(l, last):
        A.reset()
        COS = A.alloc([128, S], BF16)
        SIN = A.alloc([128, S], BF16)
        PROT = A.alloc([128, 128], BF16)
        WV = A.alloc([128, 8, 512], BF16)
        VH = A.alloc([128, NB, 512], BF16)
        WK = A.alloc([128, 8, 128], BF16)
        WQ = A.alloc([128, 8, 128], BF16)
        KT = A.alloc([128, NT], BF16)
        QT = A.alloc([128, NT], BF16)
        KB_ = [A.alloc([128, 512], BF16) for _ in range(2)]
        PT = [A.alloc([128, 2, 512], BF16) for _ in range(3)]
        TT = A.alloc([128, 4, 512], F32)
        SQ = A.alloc([128, 512], BF16)
        LNS = A.alloc([128, 512], F32)
        ATS = [A.alloc([128, 512], BF16) for _ in range(2)]
        dma_sp("ldc", COS, cos_d[:, :], writes=["COS"])
        dma_sp("ldc", SIN, sin_d[:, :], writes=["SIN"])
        dma_sp("ldc", PROT, prot_d[:, :], writes=["PROT"])
        ev = 0
        ropei = 0
        ats_i = 0
        for h in range(H):
            hh = h % 4
            bank_ctr[0] = 0
            if hh == 0:
                dma_cast("wv", WV, wview(w_in[l], OFF_V + (h // 4) * 512, 512), writes=["WV"])
                for tb in range(NB):
                    b = nbank()
                    mm(PS[:, b, :], [(UT[:, kc, tb * 128:(tb + 1) * 128], WV[:, kc, :]) for kc in range(8)],
                       reads=ut_keys(tb * 128, 128) + ["WV"], writes=[psb(b)])
                    evac_copy(ev, VH[:, tb, :], PS[:, b, :], [psb(b)], [("VH", tb)])
                    ev += 1
            dma_cast("wk", WK, wview(w_in[l], OFF_K + h * 128, 128), writes=["WK"])
            dma_cast("wq", WQ, wview(w_in[l], OFF_Q + h * 128, 128), writes=["WQ"])
            rope_pending = [None]
            for which, Wt, dst in (("K", WK, KT), ("Q", WQ, QT)):
                for ci, (t0, T) in enumerate(TCH):
                    if which == "Q" and last and t0 >= S:
                        continue
                    b = nbank()
                    mm(PS[:, b, 0:T], [(Wt[:, kc, :], UT[:, kc, t0:t0 + T]) for kc in range(8)],
                       reads=["W" + which] + ut_keys(t0, T), writes=[psb(b)])
                    if t0 < S:
                        s_ = ropei % 2
                        ropei += 1
                        ta, tb_ = TT[:, 2 * s_, :], TT[:, 2 * s_ + 1, :]
                        P.op("act", lambda e, s_=s_, b=b: e.activation(out=KB_[s_], in_=PS[:, b, :], func=AF.Copy),
                             reads=[psb(b)], writes=[("KB", s_)])
                        P.op("dve", lambda e, ta=ta, b=b, t0=t0: e.tensor_tensor(ta, PS[:, b, :], COS[:, t0:t0 + 512], ALU.mult),
                             reads=[psb(b), "COS"], writes=[("TT", 2 * s_)])
                        def rope_fin(s_=s_, ta=ta, tb_=tb_, dst=dst, t0=t0, which=which, ci=ci):
                            b2 = nbank()
                            mm(PS[:, b2, :], [(PROT, KB_[s_])], reads=["PROT", ("KB", s_)], writes=[psb(b2)])
                            P.op("dve", lambda e: e.tensor_tensor(tb_, PS[:, b2, :], SIN[:, t0:t0 + 512], ALU.mult),
                                 reads=[psb(b2), "SIN"], writes=[("TT", 2 * s_ + 1)])
                            P.op("dve", lambda e: e.tensor_tensor(dst[:, t0:t0 + 512], ta, tb_, ALU.add),
                                 reads=[("TT", 2 * s_), ("TT", 2 * s_ + 1)], writes=[(which + "T", ci)])
                        if rope_pending[0] is not None:
                            rope_pending[0]()
                        rope_pending[0] = rope_fin
                    else:
                        evac_copy(ev, dst[:, t0:t0 + T], PS[:, b, 0:T], [psb(b)], [(which + "T", ci)])
                        ev += 1
            if rope_pending[0] is not None:
                rope_pending[0]()
                rope_pending[0] = None
            qsets = [((q0, 512), list(range(NB)), qi) for qi, q0 in enumerate((0, 512, 1024, 1536))]
            if not last:
                qsets.append(((S, 256), [16, 17], 4))
            it = 0
            pti = 0
            pending = [None]
            acc_i = [0]
            ACC = [LNBC[:, 0, 0:512], LNBC[:, 1, 0:512]]

            def flush():
                if pending[0] is not None:
                    pending[0]()
                    pending[0] = None

            for (q0, Tq), kbs, qi in qsets:
                slots = {}

                def emit_s(i, kbs=kbs, q0=q0, Tq=Tq, qi=qi, slots=slots):
                    nonlocal it, pti
                    kb = kbs[i]
                    sp_ = (0, 1) if it % 2 == 0 else (2, 3)
                    it += 1
                    j = pti % 3
                    pti += 1
                    slots[i] = j

                    def sfn(e):
                        e.matmul(PS[:, sp_[0], 0:Tq], KT[0:64, kb * 128:(kb + 1) * 128], QT[0:64, q0:q0 + Tq], start=True, stop=True)
                        return e.matmul(PS[:, sp_[1], 0:Tq], KT[64:128, kb * 128:(kb + 1) * 128], QT[64:128, q0:q0 + Tq], start=True, stop=True)
                    P.op("pe", sfn, reads=[("KT", kb // 4), ("QT", qi)], writes=[psb(sp_[0]), psb(sp_[1])])
                    P.op("act", lambda e: e.activation(out=PT[j][:, :, 0:Tq], in_=PS[:, sp_[0]:sp_[0] + 2, 0:Tq],
                                                       func=AF.Exp, scale=0.125),
                         reads=[psb(sp_[0]), psb(sp_[1])], writes=[("PT", j)])

                def emit_av(i, kbs=kbs, Tq=Tq, slots=slots):
                    kb = kbs[i]
                    j = slots[i]
                    first, lastk = (i == 0), (i == len(kbs) - 1)
                    hh_ = hh

                    def afn(e):
                        v = VH[:, kb, hh_ * 128:(hh_ + 1) * 128]
                        e.matmul(PS[:, 4, 0:Tq], v, PT[j][:, 0, 0:Tq], start=first, stop=lastk)
                        e.matmul(PS[:, 5, 0:Tq], v, PT[j][:, 1, 0:Tq], start=first, stop=lastk)
                        return e.matmul(PS[:, 7, 0:Tq], ONES[:], PT[j][:, 1, 0:Tq], start=first, stop=lastk)
                    P.op("pe", afn, reads=[("VH", kb), ("PT", j), "ONES"], writes=[psb(4), psb(5), psb(7)])
                    acc = ACC[acc_i[0] % 2]
                    if first:
                        P.op("dve", lambda e, acc=acc: e.tensor_copy(acc[:, 0:Tq], PT[j][:, 0, 0:Tq]),
                             reads=[("PT", j)], writes=[("ACC", acc_i[0] % 2)])
                    else:
                        P.op("dve", lambda e, acc=acc: e.tensor_tensor(acc[:, 0:Tq], acc[:, 0:Tq], PT[j][:, 0, 0:Tq], ALU.add),
                             reads=[("PT", j), ("ACC", acc_i[0] % 2)], writes=[("ACC", acc_i[0] % 2)])
                    if lastk:
                        mm(PS[:, 6, 0:Tq], [(ONES32[:], acc[:, 0:Tq])], reads=["ONES32", ("ACC", acc_i[0] % 2)], writes=[psb(6)])
                        acc_i[0] += 1

                emit_s(0)
                for i in range(len(kbs)):
                    if i + 1 < len(kbs):
                        emit_s(i + 1)
                    emit_av(i)
                    if i == min(5, len(kbs) - 1):
                        flush()
                ta, tb_, tc, td = TT[:, 0, 0:Tq], TT[:, 1, 0:Tq], TT[:, 2, 0:Tq], TT[:, 3, 0:Tq]
                P.op("dve", lambda e, tc=tc, Tq=Tq: e.tensor_copy(tc, PS[:, 5, 0:Tq]), reads=[psb(5)], writes=[("TT", 2)])
                P.op("dve", lambda e, td=td, Tq=Tq: e.tensor_copy(td, PS[:, 4, 0:Tq]), reads=[psb(4)], writes=[("TT", 3)])
                for bnk, tx, ix in ((7, tb_, 1), (6, ta, 0)):
                    P.op("act", lambda e, bnk=bnk, tx=tx, Tq=Tq: e.activation(out=tx, in_=PS[:, bnk, 0:Tq], func=AF.Ln),
                         reads=[psb(bnk)], writes=[("TT", ix)])
                for tx, ix in ((tb_, 1), (ta, 0)):
                    P.op("act", lambda e, tx=tx: e.activation(out=tx, in_=tx, func=AF.Exp, scale=-1.0),
                         reads=[("TT", ix)], writes=[("TT", ix)])
                P.op("dve", lambda e, tc=tc, tb_=tb_: e.tensor_tensor(tc, tc, tb_, ALU.mult),
                     reads=[("TT", 2), ("TT", 1)], writes=[("TT", 2)])
                P.op("dve", lambda e, td=td, ta=ta: e.tensor_tensor(td, td, ta, ALU.mult),
                     reads=[("TT", 3), ("TT", 0)], writes=[("TT", 3)])
                P.op("dve", lambda e, td=td, tc=tc: e.scalar_tensor_tensor(td, tc, NLAM[:, 0:1], td, ALU.mult, ALU.add),
                     reads=[("TT", 2), ("TT", 3), "NLAM"], writes=[("TT", 3)])

                def tail(td=td, Tq=Tq, q0=q0, qi=qi, h=h):
                    nonlocal it, ats_i
                    sb_ = 0 if it % 2 == 0 else 2
                    it += 1
                    a_ = ats_i % 2
                    ats_i += 1
                    P.op("act", lambda e: e.activation(out=SQ[:, 0:Tq], in_=td, func=AF.Square), reads=[("TT", 3)], writes=["SQ"])
                    mm(PS[:, sb_, 0:Tq], [(ONES[:], SQ[:, 0:Tq])], reads=["ONES", "SQ"], writes=[psb(sb_), psb(sb_ + 1)])
                    P.op("act", lambda e: e.activation(out=LNS[:, 0:Tq], in_=PS[:, sb_, 0:Tq], func=AF.Ln, bias=128.0 * EPS, scale=1.0),
                         reads=[psb(sb_)], writes=["LNS"])
                    P.op("act", lambda e: e.activation(out=LNS[:, 0:Tq], in_=LNS[:, 0:Tq], func=AF.Exp, scale=-0.5),
                         reads=["LNS"], writes=["LNS"])
                    P.op("dve", lambda e: e.scalar_tensor_tensor(ATS[a_][:, 0:Tq], td, SGS[:, 0:1], LNS[:, 0:Tq], ALU.mult, ALU.mult),
                         reads=[("TT", 3), "LNS", "SGS"], writes=[("ATS", a_)])
                    dma_sp("ats%d" % a_, att_d[h, :, q0:q0 + Tq], ATS[a_][:, 0:Tq], reads=[("ATS", a_)], writes=[("ATTd", h, qi)])
                pending[0] = tail
            flush()

    def epi_a(tb, banks, gsel, TMP, tmpkeys):
        for ch in range(2):
            P.op("dve", lambda e, ch=ch: e.tensor_tensor(TMP[:, ch * 512:(ch + 1) * 512], PS[:, banks[ch], :],
                                                         GBC[:, gsel, ch * 512:(ch + 1) * 512], ALU.mult),
                 reads=[psb(banks[ch]), ("GBC", gsel, ch)], writes=tmpkeys)
        P.op("dve", lambda e: e.scalar_tensor_tensor(X[:, tb, :], X[:, tb, :], ALPHA, TMP, ALU.mult, ALU.add),
             reads=[("X", tb)] + tmpkeys, writes=[("X", tb)])

    def epi_b(tb, TMP, tmpkeys):
        P.op("dve", lambda e: [e.bn_stats(ST[:, tb, 0, :], X[:, tb, 0:512]), e.bn_stats(ST[:, tb, 1, :], X[:, tb, 512:1024])],
             reads=[("X", tb)], writes=[("ST", tb)])
        P.op("dve", lambda e: e.bn_aggr(MV[:, tb, :], ST[:, tb, :, :]), reads=[("ST", tb)], writes=[("MV", tb)])
        P.op("act", lambda e: e.activation(out=LNV[:, tb:tb + 1], in_=MV[:, tb, 1:2], func=AF.Ln, bias=EPS, scale=1.0),
             reads=[("MV", tb)], writes=[("LNV", tb)])
        P.op("act", lambda e: e.activation(out=RS[:, tb:tb + 1], in_=LNV[:, tb:tb + 1], func=AF.Exp, scale=-0.5),
             reads=[("LNV", tb)], writes=[("RS", tb)])
        P.op("dve", lambda e: e.scalar_tensor_tensor(NBI[:, tb:tb + 1], MV[:, tb, 0:1], -1.0, RS[:, tb:tb + 1], ALU.mult, ALU.mult),
             reads=[("MV", tb), ("RS", tb)], writes=[("NBI", tb)])
        P.op("act", lambda e: e.activation(out=TMP, in_=X[:, tb, :], func=AF.Identity, scale=RS[:, tb:tb + 1], bias=NBI[:, tb:tb + 1]),
             reads=[("X", tb), ("RS", tb), ("NBI", tb)], writes=tmpkeys)
        P.op("dve", lambda e: e.tensor_tensor(TMP, TMP, LNBC[:, 0, :], ALU.mult), reads=tmpkeys + ["LNBC"], writes=tmpkeys)
        P.op("dve", lambda e: e.tensor_tensor(X[:, tb, :], TMP, LNBC[:, 1, :], ALU.add), reads=tmpkeys + ["LNBC"], writes=[("X", tb)])

    def epilogue(tb, banks, gsel, TMP, tmpkeys):
        epi_a(tb, banks, gsel, TMP, tmpkeys)
        epi_b(tb, TMP, tmpkeys)

    def emit_merge(l, last):
        A.reset()
        chunks = TCH[:4] if last else TCH
        WO = A.alloc([128, 8, D], BF16)
        ATTC = A.alloc([128, 8, 512], BF16)
        YC = A.alloc([128, 4, 512], BF16)
        PC = A.alloc([128, 4, 512], BF16)
        WG3 = [A.alloc([128, 8, 3, 128], BF16) for _ in range(2)]
        WAB2 = [A.alloc([128, 8, 128], BF16) for _ in range(2)]
        WFB2 = [A.alloc([128, 4, 128], BF16) for _ in range(2)]
        WPB2 = [A.alloc([128, 4, 128], BF16) for _ in range(2)]
        SGt = [A.alloc([128, 512], F32) for _ in range(3)]
        TT = A.alloc([128, 2, 512], F32)
        MT = A.alloc([128, 8, 512], BF16)
        TMP = TT[:, 0:2, :].rearrange("p a n -> p (a n)")
        dma_cast("wo", WO, wview(w_out[l], 0, D), writes=["WO"])
        dma_sp("lnbc", LNBC[:, 0, :], ln1_g[l].partition_broadcast(128), writes=["LNBC"])
        dma_sp("lnbc", LNBC[:, 1, :], ln1_b[l].partition_broadcast(128), writes=["LNBC"])
        wgv = w_in[l][:, OFF_G:OFF_G + 3 * D].rearrange("(kc p) (i n) -> p kc i n", p=128, i=3)
        it = 0
        for ci, (t0, T) in enumerate(chunks):
            dma_sp("ldatt", ATTC[:, :, 0:T], att_d[:, :, t0:t0 + T].rearrange("h p t -> p h t"),
                   reads=[("ATTd", h, ci) for h in range(H)], writes=["ATTC"])
            dma_sp("ldy", YC[:, :, 0:T], y_d[:, :, t0:t0 + T].rearrange("g p t -> p g t"),
                   reads=[("Yd", g, ci) for g in range(4)], writes=["YC"])
            dma_sp("ldp", PC[:, :, 0:T], pool_d[:, :, t0:t0 + T].rearrange("g p t -> p g t"),
                   reads=[("Pd", g, ci) for g in range(4)], writes=["PC"])
            for fc in range(8):
                slot = it % 2
                it += 1
                WAB, WFB, WPB = WAB2[slot], WFB2[slot], WPB2[slot]
                g3flat = WG3[slot].rearrange("p a b c -> p (a b c)")
                abflat = WAB.rearrange("p a b -> p (a b)")
                fbflat = WFB.rearrange("p a b -> p (a b)")
                pbflat = WPB.rearrange("p a b -> p (a b)")
                if ci == 0:
                    P.op("pool", lambda e, slot=slot, fc=fc: [e.dma_start(out=WG3[slot][:, :, i, :], in_=wgv[:, :, i, fc * 128:(fc + 1) * 128])
                                                               for i in range(3)], writes=[("WG3", slot)], dma="wg%d" % slot, ndma=3)
                    dma_cast("wab%d" % slot, WAB, w_att_br[l][:, fc * 128:(fc + 1) * 128].rearrange("(h p) n -> p h n", p=128), writes=[("WAB", slot)])
                    dma_cast("wfb%d" % slot, WFB, w_four_br[l][:, fc * 128:(fc + 1) * 128].rearrange("(g p) n -> p g n", p=128), writes=[("WFB", slot)])
                    dma_cast("wpb%d" % slot, WPB, w_pool_br[l][:, fc * 128:(fc + 1) * 128].rearrange("(g p) n -> p g n", p=128), writes=[("WPB", slot)])
                    dma_sp("wg3s%d" % slot, wg3_scr[fc], g3flat, reads=[("WG3", slot)], writes=[("WG3d", fc)])
                    dma_sp("wabs%d" % slot, wab_scr[fc], abflat, reads=[("WAB", slot)], writes=[("WABd", fc)])
                    dma_sp("wfbs%d" % slot, wfb_scr[fc], fbflat, reads=[("WFB", slot)], writes=[("WFBd", fc)])
                    dma_sp("wpbs%d" % slot, wpb_scr[fc], pbflat, reads=[("WPB", slot)], writes=[("WPBd", fc)])
                else:
                    dma_sp("wg3l%d" % slot, g3flat, wg3_scr[fc], reads=[("WG3d", fc)], writes=[("WG3", slot)])
                    dma_sp("wabl%d" % slot, abflat, wab_scr[fc], reads=[("WABd", fc)], writes=[("WAB", slot)])
                    dma_sp("wfbl%d" % slot, fbflat, wfb_scr[fc], reads=[("WFBd", fc)], writes=[("WFB", slot)])
                    dma_sp("wpbl%d" % slot, pbflat, wpb_scr[fc], reads=[("WPBd", fc)], writes=[("WPB", slot)])
                bG = [nbank() for _ in range(3)]
                bA, bF, bP = nbank(), nbank(), nbank()
                for i in range(3):
                    mm(PS[:, bG[i], 0:T], [(WG3[slot][:, kc, i, :], UT[:, kc, t0:t0 + T]) for kc in range(8)],
                       reads=[("WG3", slot)] + ut_keys(t0, T), writes=[psb(bG[i])])
                    P.op("act", lambda e, i=i, b=bG[i], T=T: e.activation(out=SGt[i][:, 0:T], in_=PS[:, b, 0:T], func=AF.Sigmoid),
                         reads=[psb(bG[i])], writes=[("SGt", i)])
                mm(PS[:, bA, 0:T], [(WAB[:, h, :], ATTC[:, h, 0:T]) for h in range(H)], reads=[("WAB", slot), "ATTC"], writes=[psb(bA)])
                mm(PS[:, bF, 0:T], [(WFB[:, g, :], YC[:, g, 0:T]) for g in range(4)], reads=[("WFB", slot), "YC"], writes=[psb(bF)])
                mm(PS[:, bP, 0:T], [(WPB[:, g, :], PC[:, g, 0:T]) for g in range(4)], reads=[("WPB", slot), "PC"], writes=[psb(bP)])
                P.op("dve", lambda e, T=T, b=bA: e.tensor_tensor(TT[:, 0, 0:T], PS[:, b, 0:T], SGt[0][:, 0:T], ALU.mult),
                     reads=[psb(bA), ("SGt", 0)], writes=[("MTT", 0)])
                P.op("dve", lambda e, T=T, b=bF: e.tensor_tensor(TT[:, 1, 0:T], PS[:, b, 0:T], SGt[1][:, 0:T], ALU.mult),
                     reads=[psb(bF), ("SGt", 1)], writes=[("MTT", 1)])
                P.op("dve", lambda e, T=T: e.tensor_tensor(TT[:, 0, 0:T], TT[:, 0, 0:T], TT[:, 1, 0:T], ALU.add),
                     reads=[("MTT", 0), ("MTT", 1)], writes=[("MTT", 0)])
                P.op("dve", lambda e, T=T, b=bP: e.tensor_tensor(TT[:, 1, 0:T], PS[:, b, 0:T], SGt[2][:, 0:T], ALU.mult),
                     reads=[psb(bP), ("SGt", 2)], writes=[("MTT", 1)])
                P.op("dve", lambda e, T=T, fc=fc: e.tensor_tensor(MT[:, fc, 0:T], TT[:, 0, 0:T], TT[:, 1, 0:T], ALU.add),
                     reads=[("MTT", 0), ("MTT", 1)], writes=[("MT", fc)])
            for tbl in range(T // 128):
                tb = t0 // 128 + tbl
                banks = (nbank(), nbank())
                for ch in range(2):
                    mm(PS[:, banks[ch], :], [(MT[:, fc, tbl * 128:(tbl + 1) * 128], WO[:, fc, ch * 512:(ch + 1) * 512]) for fc in range(8)],
                       reads=[("MT", fc) for fc in range(8)] + ["WO"], writes=[psb(banks[ch])])
                epi_a(tb, banks, 0 if t0 < S else 2, TMP, [("MTT", 0), ("MTT", 1)])
            for tbl in range(T // 128):
                epi_b(t0 // 128 + tbl, TMP, [("MTT", 0), ("MTT", 1)])

    def emit_ffn(l, last):
        A.reset()
        passes = [(0, 1024), (1024, 1024)] + ([] if last else [(2048, 256)])
        HT = A.alloc([128, NFF, 1024], BF16)
        WGU = [A.alloc([128, 8, 2, 256], BF16) for _ in range(2)]
        WD = [A.alloc([128, 2, D], BF16),
              GBC[:, 0, :].bitcast(BF16).rearrange("p (j n) -> p j n", j=2),
              GBC[:, 2, :].bitcast(BF16).rearrange("p (j n) -> p j n", j=2)]
        SL = [A.alloc([128, 512], F32)]
        TMP = A.alloc([128, D], F32)
        dma_sp("lnbc", LNBC[:, 0, :], ln2_g[l].partition_broadcast(128), writes=["LNBC"])
        dma_sp("lnbc", LNBC[:, 1, :], ln2_b[l].partition_broadcast(128), writes=["LNBC"])
        gi = 0
        di = 0
        for (p0, TP) in passes:
            halves = [(h0, min(512, TP - h0)) for h0 in range(0, TP, 512)]
            for ffg in range(NFF // 2):
                slot = gi % 2
                gi += 1
                wflat = WGU[slot].rearrange("p a b c -> p (a b c)")
                if p0 == 0:
                    dma_cast("wgu%d" % slot, WGU[slot][:, :, 0, :], wview(w_ffn_gate[l], ffg * 256, 256), writes=[("WGU", slot, 0)])
                    dma_cast("wgu%d" % slot, WGU[slot][:, :, 1, :], wview(w_ffn_up[l], ffg * 256, 256), writes=[("WGU", slot, 1)])
                    dma_sp("wgus%d" % slot, wgu_scr[ffg], wflat, reads=[("WGU", slot, 0), ("WGU", slot, 1)], writes=[("WGUd", ffg)])
                else:
                    dma_sp("wgul%d" % slot, wflat, wgu_scr[ffg], reads=[("WGUd", ffg)], writes=[("WGU", slot, 0), ("WGU", slot, 1)])
                for j in range(2):
                    ff = ffg * 2 + j
                    for (h0, T) in halves:
                        t0 = p0 + h0
                        bg, bu = nbank(), nbank()
                        mm(PS[:, bg, 0:T], [(WGU[slot][:, kc, 0, j * 128:(j + 1) * 128], UT[:, kc, t0:t0 + T]) for kc in range(8)],
                           reads=[("WGU", slot, 0)] + ut_keys(t0, T), writes=[psb(bg)])
                        mm(PS[:, bu, 0:T], [(WGU[slot][:, kc, 1, j * 128:(j + 1) * 128], UT[:, kc, t0:t0 + T]) for kc in range(8)],
                           reads=[("WGU", slot, 1)] + ut_keys(t0, T), writes=[psb(bu)])
                        P.op("act", lambda e, bg=bg, T=T: e.activation(out=SL[0][:, 0:T], in_=PS[:, bg, 0:T], func=AF.Silu),
                             reads=[psb(bg)], writes=[("SL", 0)])
                        P.op("dve", lambda e, bu=bu, T=T, ff=ff, h0=h0: e.tensor_tensor(HT[:, ff, h0:h0 + T], SL[0][:, 0:T], PS[:, bu, 0:T], ALU.mult),
                             reads=[psb(bu), ("SL", 0)], writes=[("HT", ff, h0 // 512)])
            for r0 in range(0, TP // 128, 4):
                nb_ = min(4, TP // 128 - r0)
                banks = [(2 * t, 2 * t + 1) for t in range(nb_)]
                allb = [psb(b) for pr in banks for b in pr]
                for ffg in range(NFF // 2):
                    slot = di % 3
                    di += 1
                    wdflat = WD[slot].rearrange("p j n -> p (j n)")
                    if p0 == 0 and r0 == 0:
                        dma_cast("wd%d" % slot, WD[slot], w_ffn_down[l][ffg * 256:(ffg + 1) * 256, :].rearrange("(j p) n -> p j n", p=128),
                                 writes=[("WD", slot)])
                        dma_sp("wds%d" % slot, wd_scr[ffg], wdflat, reads=[("WD", slot)], writes=[("WDd", ffg)])
                    elif ffg % 2 == 0:
                        dma_sp("wdl%d" % slot, wdflat, wd_scr[ffg], reads=[("WDd", ffg)], writes=[("WD", slot)])
                    else:
                        P.op("pool", lambda e, wdflat=wdflat, ffg=ffg: e.dma_start(out=wdflat, in_=wd_scr[ffg]),
                             reads=[("WDd", ffg)], writes=[("WD", slot)], dma="wdp%d" % slot)

                    def fn(e, ffg=ffg, slot=slot, banks=banks, nb_=nb_, r0=r0):
                        ins = None
                        for j in range(2):
                            ff = ffg * 2 + j
                            for t in range(nb_):
                                for ch in range(2):
                                    c0 = (r0 + t) * 128
                                    ins = e.matmul(PS[:, banks[t][ch], :], HT[:, ff, c0:c0 + 128], WD[slot][:, j, ch * 512:(ch + 1) * 512],
                                                   start=(ff == 0), stop=(ff == NFF - 1))
                        return ins
                    P.op("pe", fn, reads=[("HT", ffg * 2, r0 // 4), ("HT", ffg * 2 + 1, r0 // 4), ("WD", slot)], writes=allb)
                for t in range(nb_):
                    epi_a(p0 // 128 + r0 + t, banks[t], 1 if p0 < S else 3, TMP, ["FTMP"])
                for t in range(nb_):
                    epi_b(p0 // 128 + r0 + t, TMP, ["FTMP"])
            bank_ctr[0] = 0

    def tap(name, src_ap, shape, dt, reads):
        if name in dbg:
            t = dr_now("dbg_" + name, shape, dt, "ExternalOutput")
            dbg_out[name] = t
            dma_sp("dbg", t, src_ap, reads=reads, writes=[("dbg", name)])

    order = ["mod", "ln1", "four", "pool", "attn", "merge", "ln2", "ffn"]
    upto = dbg.get("upto", "ffn")
    nph = order.index(upto) + 1
    for l in range(n_layers):
        last = l == DEPTH - 1
        steps = [
            (lambda: None) if dbg.get("skip_mod") else (lambda: emit_mod(l)),
            lambda: emit_ln_ut(0, 1, NB, reset=bool(dbg.get("skip_mod"))),
            lambda: emit_fourier(l, last),
            lambda: emit_pool(l, last),
            lambda: emit_attn(l, last),
            lambda: emit_merge(l, last),
            lambda: emit_ln_ut(2, 3, 16 if last else NB),
            lambda: emit_ffn(l, last),
        ]
        for si, st in enumerate(steps[:nph]):
            st()
            if si != 0:
                P.fence()
            if l == 0 and si == 1:
                tap("UT", UT[:], [128, 8, NT], BF16, [])
                tap("MODT", MODT[:], [128, 4, 8, 2], F32, [])
                tap("GBC", GBC[:], [128, 4, D], F32, [])
                tap("NLAM", NLAM[:], [128, 1], F32, [])
            if l == 0 and si == 5:
                tap("xmix", X[:], [128, NB, D], F32, [])
    if "xall" in dbg:
        tap("xall", X[:], [128, NB, D], F32, [])
    ov = out_d.rearrange("(tb p) d -> p tb d", p=128)
    for i in range(4):
        dma_sp("st%d" % i, ov[:, 4 * i:4 * i + 4, :], X[:, 4 * i:4 * i + 4, :], reads=[("X", t) for t in range(4 * i, 4 * i + 4)],
               writes=[("out", i)])
    P.fence()
    stats = P.emit(nc, es)
    es.close()
    nc._declared_inputs = set(declared)
    return nc, stats, (dbg_out, dict(att=att_d, y=y_d, pool=pool_d))


_CONSTS = None
_NC_CACHE = {}


def make_in_maps(inputs):
    global _CONSTS
    if _CONSTS is None:
        _CONSTS = make_consts()
    f = lambda a: np.ascontiguousarray(np.asarray(a, np.float32))
    shared = {
        "w_mod": f(inputs["w_mod"]), "b_mod": f(inputs["b_mod"]),
        "b_modT": f(np.asarray(inputs["b_mod"], np.float32).reshape(DEPTH, 48, 128).transpose(0, 2, 1)),
        "w_in": f(inputs["w_in"]), "lam_qk": f(np.asarray(inputs["lam_qk"], np.float32).reshape(DEPTH, 256)),
        "subln_gT": f(np.asarray(inputs["subln_g"], np.float32).reshape(DEPTH, 128, 1)),
        "w_att_br": f(inputs["w_att_br"]), "w_four_br": f(inputs["w_four_br"]), "w_pool_grp": f(inputs["w_pool_grp"]),
        "pool_scaleT": f(np.asarray(inputs["pool_scale"], np.float32).reshape(DEPTH, 4, 128).transpose(0, 2, 1)),
        "w_pool_br": f(inputs["w_pool_br"]), "w_out": f(inputs["w_out"]),
        "ln1_g": f(inputs["ln1_g"]), "ln1_b": f(inputs["ln1_b"]),
        "w_ffn_gate": f(inputs["w_ffn_gate"]), "w_ffn_up": f(inputs["w_ffn_up"]), "w_ffn_down": f(inputs["w_ffn_down"]),
        "ln2_g": f(inputs["ln2_g"]), "ln2_b": f(inputs["ln2_b"]),
    }
    shared.update(_CONSTS)
    x = np.asarray(inputs["x"], np.float32)
    c = np.asarray(inputs["c"], np.float32)
    ctx = np.asarray(inputs["ctx"], np.float32)
    cc = np.asarray(inputs["c_ctx"], np.float32)
    maps = []
    for b in range(x.shape[0]):
        m = dict(shared)
        m["x"] = f(x[b])
        m["ctx"] = f(ctx[b])
        m["cT"] = f(np.concatenate([c[b].reshape(8, 128).T, cc.reshape(8, 128).T], axis=1))
        maps.append(m)
    return maps


def kernel(**inputs):
    if "full" not in _NC_CACHE:
        _NC_CACHE["full"] = build_program(DEPTH)[0]
    nc = _NC_CACHE["full"]
    maps = [{k: v for k, v in m.items() if k in nc._declared_inputs} for m in make_in_maps(inputs)]
    res = run_bass_kernel_spmd(nc, maps, core_ids=list(range(len(maps))))
    out = np.stack([np.asarray(r["out"], np.float32).reshape(S, D) for r in res.results], axis=0)
    return out
```

```python
import math
from contextlib import ExitStack

import ml_dtypes
import numpy as np

import concourse.bass as bass
import concourse.mybir as mybir
from concourse.bass_utils import run_bass_kernel_spmd

F32 = mybir.dt.float32
BF16 = mybir.dt.bfloat16
AF = mybir.ActivationFunctionType
ALU = mybir.AluOpType
AX = mybir.AxisListType

D = 1024
S = 2048
LC = 256
NT = S + LC
NB = NT // 128
DEPTH = 4
H = 8
N_IN = 7168
OFF_K, OFF_V, OFF_Q, OFF_F, OFF_P, OFF_G = 0, 1024, 2048, 3072, 3584, 4096
DFF = 2816
NFF = DFF // 128
ALPHA = (2 * DEPTH) ** 0.25
EPS = 1e-5
GRID_W = 64

ENGS = ("pe", "act", "dve", "pool", "sp")


class Prog:
    def __init__(self):
        self.ops = {e: [] for e in ENGS}
        self.lastw = {}
        self.readers = {}
        self.dma_cnt = {}
        self.dma_last = {}
        self.sig = {e: set() for e in ENGS}

    @staticmethod
    def _merge(deps, tok):
        k = (tok[0], tok[1])
        if deps.get(k, -1) < tok[2]:
            deps[k] = tok[2]

    def op(self, eng, fn, reads=(), writes=(), dma=None, ndma=1):
        idx = len(self.ops[eng])
        deps = {}
        for r in reads:
            w = self.lastw.get(r)
            if w is not None:
                self._merge(deps, w)
            if isinstance(r, tuple) and r[0] == "ps":
                for k, v in self.readers.get(r, {}).items():
                    if k[1] != eng:
                        self._merge(deps, (k[0], k[1], v))
        for r in writes:
            w = self.lastw.get(r)
            if w is not None:
                self._merge(deps, w)
            for k, v in self.readers.get(r, {}).items():
                self._merge(deps, (k[0], k[1], v))
        rec = dict(fn=fn, deps=deps, dma=dma, ndma=ndma)
        if dma is not None:
            prev = self.dma_last.get(dma)
            if prev is not None:
                self._merge(deps, prev)
            self.dma_cnt[dma] = self.dma_cnt.get(dma, 0) + ndma
            tok = ("d", dma, 16 * self.dma_cnt[dma])
            self.dma_last[dma] = tok
        else:
            tok = ("c", eng, idx)
        deps.pop(("c", eng, idx), None)
        for k in deps:
            if k[0] == "c":
                self.sig[k[1]].add(deps[k])
        self.ops[eng].append(rec)
        for r in reads:
            rd = self.readers.setdefault(r, {})
            k = (tok[0], tok[1])
            if rd.get(k, -1) < tok[2]:
                rd[k] = tok[2]
        for r in writes:
            self.lastw[r] = tok
            self.readers[r] = {}
        return tok

    def fence(self):
        deps = {}
        for e in ENGS:
            real = [i for i, r in enumerate(self.ops[e]) if r["fn"] is not None and r["dma"] is None]
            if real:
                deps[("c", e)] = real[-1]
        for k, tok in self.dma_last.items():
            deps[("d", k)] = tok[2]
        for e in ENGS:
            d = dict(deps)
            idx = len(self.ops[e])
            d.pop(("c", e), None) if False else None
            for k in d:
                if k[0] == "c":
                    self.sig[k[1]].add(d[k])
            self.ops[e].append(dict(fn=None, deps=d, dma=None, ndma=0))
        self.lastw = {}
        self.readers = {}

    def emit(self, nc, es):
        esem = {e: es.enter_context(nc.semaphore("s_" + e)) for e in ENGS}
        dsem = {k: es.enter_context(nc.semaphore("d_%d" % i)) for i, k in enumerate(self.dma_cnt)}
        sigcount = {}
        for e in ENGS:
            c = 0
            m = {}
            for i in range(len(self.ops[e])):
                if i in self.sig[e]:
                    c += 1
                    m[i] = c
            sigcount[e] = m
        block = es.enter_context(nc.Block())
        stats = {}

        def make(e):
            def body(eng):
                waited = {}
                nw = 0
                for i, rec in enumerate(self.ops[e]):
                    for k, v in rec["deps"].items():
                        if k[0] == "c":
                            if k[1] == e and e == "pe":
                                continue
                            if k[1] == e and v >= i:
                                continue
                            sem = esem[k[1]]
                            val = sigcount[k[1]][v]
                            wk = ("c", k[1])
                        else:
                            sem = dsem[k[1]]
                            val = v
                            wk = ("d", k[1])
                        if waited.get(wk, 0) >= val:
                            continue
                        waited[wk] = val
                        eng.wait_ge(sem, val)
                        nw += 1
                    if rec["fn"] is None:
                        continue
                    ins = rec["fn"](eng)
                    if rec["dma"] is not None:
                        if not isinstance(ins, (list, tuple)):
                            ins = [ins]
                        assert len(ins) == rec["ndma"], (len(ins), rec["ndma"])
                        for x in ins:
                            x.then_inc(dsem[rec["dma"]], 16)
                    elif i in self.sig[e]:
                        if isinstance(ins, (list, tuple)):
                            ins = ins[-1]
                        ins.then_inc(esem[e], 1)
                stats[e] = (len(self.ops[e]), nw)
            return body

        block.tensor(make("pe"))
        block.scalar(make("act"))
        block.vector(make("dve"))
        block.gpsimd(make("pool"))
        block.sync(make("sp"))
        return stats


def _bf(a):
    return np.ascontiguousarray(np.asarray(a, np.float32)).astype(ml_dtypes.bfloat16)


def make_consts():
    c = {}
    c["ident"] = _bf(np.eye(128))
    t = np.arange(S)
    row = (t // GRID_W).astype(np.float64)
    col = (t % GRID_W).astype(np.float64)
    inv = 10000.0 ** (-np.arange(16, dtype=np.float64) / 16)
    ang = np.zeros((64, S))
    for d in range(64):
        axis, f = d // 32, d % 16
        ang[d] = (row if axis == 0 else col) * inv[f]
    cos = np.cos(ang)
    sin = np.sin(ang)
    c["rope_cos"] = _bf(np.concatenate([cos, cos], 0))
    c["rope_sin"] = _bf(np.concatenate([sin, sin], 0))
    pr = np.zeros((128, 128))
    for m in range(2):
        for d in range(64):
            half = (d % 32) // 16
            if half == 0:
                pr[m * 64 + d + 16, m * 64 + d] = -1.0
            else:
                pr[m * 64 + d - 16, m * 64 + d] = 1.0
    c["prot"] = _bf(pr)
    k = np.arange(128)
    a = 2 * np.pi * np.outer(k, k) / 128
    c["ccsc"] = _bf(np.concatenate([np.cos(a), np.sin(a)], 1))
    l = np.arange(S)
    a = 2 * np.pi * ((np.outer(l, l)) % S) / S
    c["dft"] = _bf(np.stack([np.cos(a), -np.sin(a)], 0))
    l = np.arange(LC)
    a = 2 * np.pi * ((np.outer(l, l)) % LC) / LC
    c["dft256"] = _bf(np.stack([np.cos(a), -np.sin(a)], 0))
    L = 384
    bands = np.zeros((128, 4, 5, 128))
    for g, w in enumerate((2, 4, 8, 16)):
        lo = w // 2
        hi = w - lo
        M = np.zeros((L, L))
        for tp in range(L):
            s0 = min(max(tp - lo, 0), L)
            e0 = min(max(tp + hi, 0), L)
            M[s0:e0, tp] = 1.0 / (e0 - s0)
            M[tp, tp] -= 1.0
        bands[:, g, 0] = M[0:128, 128:256]
        bands[:, g, 1] = M[128:256, 128:256]
        bands[:, g, 2] = M[256:384, 128:256]
        bands[:, g, 3] = M[0:128, 0:128]
        bands[:, g, 4] = M[256:384, 256:384]
    c["bands"] = _bf(bands.reshape(128, 20, 128))
    return c


TCH = [(0, 512), (512, 512), (1024, 512), (1536, 512), (2048, 256)]
ARENA_BYTES = 70 * 1024


def build_program(n_layers=DEPTH, dbg=None):
    dbg = dbg or {}
    nc = bass.Bass("TRN2", target_bir_lowering=False)
    P = Prog()
    es = ExitStack()

    declared = []

    def dr_now(name, shape, dt, kind="ExternalInput"):
        declared.append(name)
        return nc.dram_tensor(name, list(shape), dt, kind=kind).ap()

    class Lazy:
        def __init__(self, name, shape, dt, kind):
            self.args = (name, shape, dt, kind)
            self._ap = None

        def ap(self):
            if self._ap is None:
                self._ap = dr_now(*self.args)
            return self._ap

        def __getitem__(self, k):
            return self.ap()[k]

        def rearrange(self, *a, **kw):
            return self.ap().rearrange(*a, **kw)

    def dr(name, shape, dt, kind="ExternalInput"):
        if kind == "ExternalInput":
            return Lazy(name, shape, dt, kind)
        return dr_now(name, shape, dt, kind)

    x_d = dr("x", [S, D], F32)
    ctx_d = dr("ctx", [LC, D], F32)
    cT_d = dr("cT", [128, 16], F32)
    w_mod = dr("w_mod", [DEPTH, D, 6 * D], F32)
    b_mod = dr("b_mod", [DEPTH, 6 * D], F32)
    b_modT = dr("b_modT", [DEPTH, 128, 48], F32)
    w_in = dr("w_in", [DEPTH, D, N_IN], F32)
    lam_qk = dr("lam_qk", [DEPTH, 256], F32)
    subln_gT = dr("subln_gT", [DEPTH, 128, 1], F32)
    w_att_br = dr("w_att_br", [DEPTH, D, D], F32)
    w_four_br = dr("w_four_br", [DEPTH, 512, D], F32)
    w_pool_grp = dr("w_pool_grp", [DEPTH, 4, 128, 128], F32)
    pool_scaleT = dr("pool_scaleT", [DEPTH, 128, 4], F32)
    w_pool_br = dr("w_pool_br", [DEPTH, 512, D], F32)
    w_out = dr("w_out", [DEPTH, D, D], F32)
    ln1_g = dr("ln1_g", [DEPTH, D], F32)
    ln1_b = dr("ln1_b", [DEPTH, D], F32)
    w_ffn_gate = dr("w_ffn_gate", [DEPTH, D, DFF], F32)
    w_ffn_up = dr("w_ffn_up", [DEPTH, D, DFF], F32)
    w_ffn_down = dr("w_ffn_down", [DEPTH, DFF, D], F32)
    ln2_g = dr("ln2_g", [DEPTH, D], F32)
    ln2_b = dr("ln2_b", [DEPTH, D], F32)
    ident_d = dr("ident", [128, 128], BF16)
    cos_d = dr("rope_cos", [128, S], BF16)
    sin_d = dr("rope_sin", [128, S], BF16)
    prot_d = dr("prot", [128, 128], BF16)
    ccsc_d = dr("ccsc", [128, 256], BF16)
    dft_d = dr("dft", [2, S, S], BF16)
    dft256_d = dr("dft256", [2, LC, LC], BF16)
    bands_d = dr("bands", [128, 20, 128], BF16)
    out_d = dr("out", [S, D], F32, "ExternalOutput")
    skind = "ExternalOutput" if dbg.get("scr_out") else "Internal"
    att_d = dr("att_scr", [H, 128, NT], BF16, skind)
    y_d = dr("y_scr", [4, 128, NT], BF16, skind)
    pool_d = dr("pool_scr", [4, 128, NT], BF16, skind)
    wgu_scr = dr("wgu_scr", [NFF // 2, 128, 4096], BF16, "Internal")
    wd_scr = dr("wd_scr", [NFF // 2, 128, 2 * D], BF16, "Internal")
    wg3_scr = dr("wg3_scr", [8, 128, 3072], BF16, "Internal")
    wab_scr = dr("wab_scr", [8, 128, 1024], BF16, "Internal")
    wfb_scr = dr("wfb_scr", [8, 128, 512], BF16, "Internal")
    wpb_scr = dr("wpb_scr", [8, 128, 512], BF16, "Internal")
    dbg_out = {}

    def sb(name, shape, dt):
        return es.enter_context(nc.sbuf_tensor(name, list(shape), dt))

    X = sb("X", [128, NB, D], F32)
    UT = sb("UT", [128, 8, NT], BF16)
    IDN = sb("IDN", [128, 128], BF16)
    ONES = sb("ONES", [128, 128], BF16)
    ONES32 = sb("ONES32", [128, 128], F32)
    GBC = sb("GBC", [128, 4, D], F32)
    LNBC = sb("LNBC", [128, 2, D], F32)
    MODT = sb("MODT", [128, 4, 8, 2], F32)
    BMT = sb("BMT", [128, 48], F32)
    CT = sb("CT", [128, 16], F32)
    SC = sb("SC", [128, 16], F32)
    S2 = sb("S2", [128, 8, 2], BF16)
    ST = sb("ST", [128, NB, 2, 6], F32)
    MV = sb("MV", [128, NB, 2], F32)
    LNV = sb("LNV", [128, NB], F32)
    RS = sb("RS", [128, NB], F32)
    NBI = sb("NBI", [128, NB], F32)
    LQ = sb("LQ", [128, 256], F32)
    LQP = sb("LQP", [128, 2, 64], F32)
    LQS = sb("LQS", [128, 2], F32)
    LQE = sb("LQE", [128, 2], F32)
    NLAM = sb("NLAM", [128, 1], F32)
    SGS = sb("SGS", [128, 1], F32)
    PSC = sb("PSC", [128, 4], F32)
    ARENA = sb("ARENA", [128, ARENA_BYTES // 2], BF16)
    PS = es.enter_context(nc.psum_tensor("PS", [128, 8, 512], F32))

    class Arena:
        def __init__(self):
            self.off = 0

        def reset(self):
            self.off = 0

        def alloc(self, shape, dt):
            n = 1
            for s_ in shape[1:]:
                n *= s_
            nbytes = n * (4 if dt == F32 else 2)
            nbytes = (nbytes + 63) // 64 * 64
            assert self.off + nbytes <= ARENA_BYTES, ("arena overflow", self.off, nbytes)
            ap = ARENA[:, self.off // 2:(self.off + nbytes) // 2]
            self.off += nbytes
            if dt == F32:
                ap = ap.bitcast(F32)
            ap = ap[:, 0:n]
            if len(shape) == 2:
                return ap
            names = " ".join("d%d" % i for i in range(1, len(shape)))
            kw = {"d%d" % i: shape[i] for i in range(1, len(shape))}
            return ap.rearrange("p (%s) -> p %s" % (names, names), **kw)

    A = Arena()
    bank_ctr = [0]

    def nbank():
        b = bank_ctr[0] % 8
        bank_ctr[0] += 1
        return b

    def psb(b):
        return ("ps", b)

    def mm(out, pairs, reads, writes):
        pairs = list(pairs)

        def fn(e):
            n = len(pairs)
            ins = None
            for i, (l_, r_) in enumerate(pairs):
                ins = e.matmul(out, l_, r_, start=(i == 0), stop=(i == n - 1))
            return ins
        P.op("pe", fn, reads=reads, writes=writes)

    def ut_keys(t0, T):
        return [("UT", tb, kc) for tb in range(t0 // 128, (t0 + T) // 128) for kc in range(8)]

    def dma_cast(key, out, in_, reads=(), writes=(), n=1):
        P.op("pool", lambda e: e.dma_start(out=out, in_=in_), reads=reads, writes=writes, dma=key)

    def dma_sp(key, out, in_, reads=(), writes=()):
        P.op("sp", lambda e: e.dma_start(out=out, in_=in_), reads=reads, writes=writes, dma=key)

    def wview(w2d, c0, ncols):
        return w2d[:, c0:c0 + ncols].rearrange("(kc p) n -> p kc n", p=128)

    xv = x_d.rearrange("(tb p) d -> p tb d", p=128)
    for i in range(4):
        dma_sp("ldx%d" % i, X[:, 4 * i:4 * i + 4, :], xv[:, 4 * i:4 * i + 4, :],
               writes=[("X", t) for t in range(4 * i, 4 * i + 4)])
    dma_sp("ldx4", X[:, 16:18, :], ctx_d.rearrange("(tb p) d -> p tb d", p=128), writes=[("X", 16), ("X", 17)])
    dma_sp("ldc", IDN[:], ident_d[:, :], writes=["IDN"])
    dma_sp("ldc", CT[:], cT_d[:, :], writes=["CT"])
    P.op("dve", lambda e: e.memset(ONES[:], 1.0), writes=["ONES"])
    P.op("dve", lambda e: e.memset(ONES32[:], 1.0), writes=["ONES32"])
    P.op("act", lambda e: e.activation(out=SC[:], in_=CT[:], func=AF.Silu), reads=["CT"], writes=["SC"])
    P.op("dve", lambda e: e.tensor_copy(S2[:, :, 0], SC[:, 0:8]), reads=["SC"], writes=["S2"])
    P.op("dve", lambda e: e.tensor_copy(S2[:, :, 1], SC[:, 8:16]), reads=["SC"], writes=["S2"])

    def emit_mod(l):
        A.reset()
        SREP = A.alloc([128, 2, 8, 128], BF16)
        for lc in range(2):
            P.op("dve", lambda e, lc=lc: e.tensor_copy(SREP[:, lc], SC[:, 8 * lc:8 * lc + 8].unsqueeze(2).to_broadcast([128, 8, 128])),
                 reads=["SC"], writes=["SREP"])
        WM = [A.alloc([128, 8, 512], BF16) for _ in range(2)]
        BR = [A.alloc([128, 512], F32) for _ in range(2)]
        dma_sp("ldm", BMT[:], b_modT[l], writes=["BMT"])
        for j in range(12):
            sec = j // 2
            slot = j % 2
            dma_cast("wm%d" % slot, WM[slot], wview(w_mod[l], j * 512, 512), writes=[("WM", slot)])
            if sec in (2, 5):
                gi = 0 if sec == 2 else 1
                dma_sp("br%d" % slot, BR[slot], b_mod[l, j * 512:(j + 1) * 512].partition_broadcast(128),
                       writes=[("BR", slot)])
                for lc in range(2):
                    b = nbank()
                    mm(PS[:, b, :], [(SREP[:, lc, kc, :], WM[slot][:, kc, :]) for kc in range(8)],
                       reads=["SREP", ("WM", slot)], writes=[psb(b)])
                    P.op("dve", lambda e, b=b, lc=lc, gi=gi, slot=slot, j=j: e.tensor_tensor(
                        GBC[:, 2 * lc + gi, (j % 2) * 512:(j % 2) * 512 + 512], PS[:, b, :], BR[slot], ALU.add),
                        reads=[psb(b), ("BR", slot)], writes=[("GBC", 2 * lc + gi, j % 2)])
            else:
                s_ = {0: 0, 1: 1, 3: 2, 4: 3}[sec]
                b = nbank()
                for f in range(4):
                    mm(PS[:, b, 2 * f:2 * f + 2], [(WM[slot][:, kc, f * 128:(f + 1) * 128], S2[:, kc, :]) for kc in range(8)],
                       reads=["S2", ("WM", slot)], writes=[psb(b)])
                h0 = (j % 2) * 4
                P.op("dve", lambda e, b=b, s_=s_, h0=h0, j=j: e.tensor_tensor(
                    MODT[:, s_, h0:h0 + 4, :], PS[:, b, 0:8].rearrange("p (f c) -> p f c", c=2),
                    BMT[:, j * 4:j * 4 + 4].unsqueeze(2).to_broadcast([128, 4, 2]), ALU.add),
                    reads=[psb(b), "BMT"], writes=[("MODT", s_, j % 2)])
                if s_ in (1, 3):
                    P.op("dve", lambda e, s_=s_, h0=h0: e.tensor_scalar(
                        MODT[:, s_, h0:h0 + 4, :], MODT[:, s_, h0:h0 + 4, :], 1.0, None, ALU.add),
                        reads=[("MODT", s_, j % 2)], writes=[("MODT", s_, j % 2)])
        lam_init = 0.8 - 0.6 * math.exp(-0.3 * l)
        dma_sp("ldm", LQ[:], lam_qk[l].partition_broadcast(128), writes=["LQ"])
        LQv = LQ[:].rearrange("p (a b d) -> p a b d", a=2, b=2)
        P.op("dve", lambda e: e.tensor_tensor(LQP[:], LQv[:, :, 0, :], LQv[:, :, 1, :], ALU.mult), reads=["LQ"], writes=["LQP"])
        P.op("dve", lambda e: e.reduce_sum(LQS[:], LQP[:], AX.X), reads=["LQP"], writes=["LQS"])
        P.op("act", lambda e: e.activation(out=LQE[:], in_=LQS[:], func=AF.Exp), reads=["LQS"], writes=["LQE"])
        P.op("dve", lambda e: e.tensor_tensor(NLAM[:], LQE[:, 1:2], LQE[:, 0:1], ALU.subtract), reads=["LQE"], writes=["NLAM"])
        P.op("dve", lambda e: e.tensor_scalar(NLAM[:], NLAM[:], -lam_init, None, ALU.add), reads=["NLAM"], writes=["NLAM"])
        dma_sp("ldm", SGS[:], subln_gT[l], writes=["SGS"])
        P.op("dve", lambda e: e.tensor_scalar(SGS[:], SGS[:], (1.0 - lam_init) * math.sqrt(128.0), None, ALU.mult),
             reads=["SGS"], writes=["SGS"])
        dma_sp("ldm", PSC[:], pool_scaleT[l], writes=["PSC"])

    def emit_ln_ut(s_sh, s_sc, nblk, reset=True):
        if reset:
            A.reset()
        XH = [A.alloc([128, D], BF16) for _ in range(3)]
        lnst = dbg.get("ln_stage", 9)

        def stats_pair(tp):
            for tb in (2 * tp, 2 * tp + 1):
                P.op("dve", lambda e, tb=tb: [e.bn_stats(ST[:, tb, 0, :], X[:, tb, 0:512]),
                                              e.bn_stats(ST[:, tb, 1, :], X[:, tb, 512:1024])],
                     reads=[("X", tb)], writes=[("ST", tb)])
                P.op("dve", lambda e, tb=tb: e.bn_aggr(MV[:, tb, :], ST[:, tb, :, :]), reads=[("ST", tb)], writes=[("MV", tb)])
            a, b_ = 2 * tp, 2 * tp + 2
            mvk = [("MV", a), ("MV", a + 1)]
            P.op("act", lambda e: e.activation(out=LNV[:, a:b_], in_=MV[:, a:b_, 1], func=AF.Ln, bias=EPS, scale=1.0),
                 reads=mvk, writes=[("LNV", tp)])
            P.op("act", lambda e: e.activation(out=RS[:, a:b_], in_=LNV[:, a:b_], func=AF.Exp, scale=-0.5),
                 reads=[("LNV", tp)], writes=[("RS", a), ("RS", a + 1)])
            P.op("dve", lambda e: e.scalar_tensor_tensor(NBI[:, a:b_], MV[:, a:b_, 0], -1.0, RS[:, a:b_], ALU.mult, ALU.mult),
                 reads=mvk + [("RS", a), ("RS", a + 1)], writes=[("NBI", a), ("NBI", a + 1)])

        stats_pair(0)
        for tp in range(nblk // 2):
            b0 = 2 * (tp % 4)
            lc = 1 if tp * 2 >= 16 else 0
            for i in range(2):
                tb = 2 * tp + i
                xh = XH[tb % 3]
                P.op("act", lambda e, tb=tb, xh=xh: e.activation(out=xh, in_=X[:, tb, :], func=AF.Identity,
                                                               scale=RS[:, tb:tb + 1], bias=NBI[:, tb:tb + 1]),
                     reads=[("X", tb), ("RS", tb), ("NBI", tb)], writes=[("XH", tb % 3)])
                pv = PS[:, b0 + i, :].bitcast(BF16)

                def tfn(e, xh=xh, pv=pv):
                    ins = None
                    for kc in range(8):
                        ins = e.transpose(pv[:, kc * 128:(kc + 1) * 128], xh[:, kc * 128:(kc + 1) * 128], IDN[:])
                    return ins
                P.op("pe", tfn, reads=[("XH", tb % 3), "IDN"], writes=[psb(b0 + i)])
            if tp + 1 < nblk // 2:
                stats_pair(tp + 1)
            pv2 = PS[:, b0:b0 + 2, :].bitcast(BF16)
            for kc in range(8):
                src = pv2[:, :, kc * 128:(kc + 1) * 128]
                dst = UT[:, kc, 2 * tp * 128:(2 * tp + 2) * 128].rearrange("p (b n) -> p b n", b=2)
                sc_ap = MODT[:, s_sc, kc, lc:lc + 1]
                sh_ap = MODT[:, s_sh, kc, lc:lc + 1]
                rd = [psb(b0), psb(b0 + 1), ("MODT", s_sc, kc // 4), ("MODT", s_sh, kc // 4)]
                wr = [("UT", 2 * tp, kc), ("UT", 2 * tp + 1, kc)]
                if tp % 2 == 0:
                    P.op("act", lambda e, dst=dst, src=src, sc_ap=sc_ap, sh_ap=sh_ap: e.activation(
                        out=dst, in_=src, func=AF.Identity, scale=sc_ap, bias=sh_ap), reads=rd, writes=wr)
                else:
                    P.op("dve", lambda e, dst=dst, src=src, sc_ap=sc_ap, sh_ap=sh_ap: e.tensor_scalar(
                        dst, src, sc_ap, sh_ap, ALU.mult, ALU.add), reads=rd, writes=wr)

    def evac_copy(i, dst, src, reads, writes, scale=None):
        if i % 2 == 0:
            if scale is None:
                P.op("act", lambda e: e.activation(out=dst, in_=src, func=AF.Copy), reads=reads, writes=writes)
            else:
                P.op("act", lambda e: e.activation(out=dst, in_=src, func=AF.Copy, scale=scale), reads=reads, writes=writes)
        else:
            if scale is None:
                P.op("dve", lambda e: e.tensor_copy(dst, src), reads=reads, writes=writes)
            else:
                P.op("dve", lambda e: e.tensor_scalar(dst, src, scale, None, ALU.mult), reads=reads, writes=writes)

    def emit_fourier(l, last):
        A.reset()
        chunks = TCH[:4] if last else TCH
        nblk = 16 if last else NB
        CCSC = A.alloc([128, 256], BF16)
        D256 = A.alloc([128, 2, 2, 256], BF16)
        WF = [A.alloc([128, 8, 256], BF16) for _ in range(2)]
        FT = [A.alloc([128, NT], BF16) for _ in range(2)]
        ABt = A.alloc([128, 2, NB, 256], BF16)
        DS = [A.alloc([128, 4, 2, 512], BF16) for _ in range(2)]
        YS = [A.alloc([128, 512], BF16) for _ in range(2)]
        dma_sp("ldc", CCSC, ccsc_d[:, :], writes=["CCSC"])
        if not last:
            for cs in range(2):
                dma_sp("ldc", D256[:, :, cs, :], dft256_d[cs].rearrange("(lb p) n -> p lb n", p=128), writes=["D256"])
        ev = 0
        for gp in range(2):
            dma_cast("wf%d" % gp, WF[gp], wview(w_in[l], OFF_F + gp * 256, 256), writes=[("WF", gp)])
            for g2 in range(2):
                for (t0, T) in chunks:
                    b = nbank()
                    mm(PS[:, b, 0:T], [(WF[gp][:, kc, g2 * 128:(g2 + 1) * 128], UT[:, kc, t0:t0 + T]) for kc in range(8)],
                       reads=[("WF", gp)] + ut_keys(t0, T), writes=[psb(b)])
                    evac_copy(ev, FT[g2][:, t0:t0 + T], PS[:, b, 0:T], [psb(b)], [("FT", g2, t0 // 512)])
                    ev += 1
            for tb in range(nblk):
                b = nbank()
                for g2 in range(2):
                    mm(PS[:, b, g2 * 256:(g2 + 1) * 256], [(FT[g2][:, tb * 128:(tb + 1) * 128], CCSC)],
                       reads=[("FT", g2, tb // 4), "CCSC"], writes=[psb(b)])
                evac_copy(ev, ABt[:, :, tb, :], PS[:, b, :].rearrange("p (g n) -> p g n", g=2), [psb(b)], [("AB", tb)])
                ev += 1
            for lq in range(4):
                bY = [nbank(), nbank()]
                for lbg in range(4):
                    slot = (lq * 4 + lbg) % 2
                    for cs, q_ in ((0, "sp"), (1, "pool")):
                        P.op(q_, lambda e, slot=slot, lbg=lbg, lq=lq, cs=cs: e.dma_start(
                            out=DS[slot][:, :, cs, :],
                            in_=dft_d[cs, lbg * 512:(lbg + 1) * 512, lq * 512:(lq + 1) * 512].rearrange("(lb p) n -> p lb n", p=128)),
                            writes=[("DS", slot, cs)], dma="ds%d_%d" % (slot, cs))

                    def fn(e, lbg=lbg, slot=slot, bY=bY):
                        ins = None
                        for lbi in range(4):
                            lb = lbg * 4 + lbi
                            for cs in range(2):
                                for g2 in range(2):
                                    ins = e.matmul(PS[:, bY[g2], :], ABt[:, g2, lb, cs * 128:(cs + 1) * 128], DS[slot][:, lbi, cs, :],
                                                   start=(lb == 0 and cs == 0), stop=(lb == 15 and cs == 1))
                        return ins
                    P.op("pe", fn, reads=[("AB", lbg * 4 + i) for i in range(4)] + [("DS", slot, 0), ("DS", slot, 1)],
                         writes=[psb(bY[0]), psb(bY[1])])
                for g2 in range(2):
                    evac_copy(ev, YS[g2], PS[:, bY[g2], :], [psb(bY[g2])], [("YS", g2)], scale=1.0 / 512.0)
                    ev += 1
                    dma_sp("ys%d" % g2, y_d[gp * 2 + g2, :, lq * 512:(lq + 1) * 512], YS[g2], reads=[("YS", g2)],
                           writes=[("Yd", gp * 2 + g2, lq)])
            if not last:
                for g2 in range(2):
                    b = nbank()
                    mm(PS[:, b, 0:256], [(ABt[:, g2, 16 + lb, cs * 128:(cs + 1) * 128], D256[:, lb, cs, :])
                                         for lb in range(2) for cs in range(2)],
                       reads=[("AB", 16), ("AB", 17), "D256"], writes=[psb(b)])
                    evac_copy(ev, YS[g2][:, 0:256], PS[:, b, 0:256], [psb(b)], [("YS", g2)], scale=1.0 / math.sqrt(256.0 * 128.0))
                    ev += 1
                    dma_sp("ys%d" % g2, y_d[gp * 2 + g2, :, S:NT], YS[g2][:, 0:256], reads=[("YS", g2)],
                           writes=[("Yd", gp * 2 + g2, 4)])

    def emit_pool(l, last):
        A.reset()
        chunks = TCH[:4] if last else TCH
        nblk = 16 if last else NB
        WP = A.alloc([128, 8, 512], BF16)
        PTOK = A.alloc([128, NB, 512], BF16)
        BND = A.alloc([128, 20, 128], BF16)
        WPG = A.alloc([128, 4, 128], BF16)
        DT = [A.alloc([128, 512], BF16) for _ in range(4)]
        PST = [A.alloc([128, 512], BF16) for _ in range(4)]
        dma_cast("wp", WP, wview(w_in[l], OFF_P, 512), writes=["WP"])
        dma_sp("ldc", BND, bands_d[:, :, :], writes=["BND"])
        dma_cast("wpg", WPG, w_pool_grp[l].rearrange("g c d -> c g d"), writes=["WPG"])
        ev = 0
        for tb in range(nblk):
            b = nbank()
            mm(PS[:, b, :], [(UT[:, kc, tb * 128:(tb + 1) * 128], WP[:, kc, :]) for kc in range(8)],
               reads=ut_keys(tb * 128, 128) + ["WP"], writes=[psb(b)])
            evac_copy(ev, PTOK[:, tb, :], PS[:, b, :], [psb(b)], [("PTOK", tb)])
            ev += 1
        for (t0, T) in chunks:
            first, lastb = (0, 15) if t0 < S else (16, 17)
            obs = list(range(t0 // 128, (t0 + T) // 128))
            rb = [("PTOK", tb) for tb in range(max(first, obs[0] - 1), min(lastb, obs[-1] + 1) + 1)]
            bb = [nbank() for _ in range(4)]
            for g in range(4):
                b = bb[g]

                def fn(e, g=g, b=b, obs=obs, first=first, lastb=lastb, t0=t0):
                    ins = None
                    for ob in obs:
                        srcs = []
                        if ob > first:
                            srcs.append((ob - 1, 0))
                        srcs.append((ob, 3 if ob == first else (4 if ob == lastb else 1)))
                        if ob < lastb:
                            srcs.append((ob + 1, 2))
                        for i, (ib, var) in enumerate(srcs):
                            ins = e.matmul(PS[:, b, ob * 128 - t0:ob * 128 - t0 + 128], PTOK[:, ib, g * 128:(g + 1) * 128],
                                           BND[:, g * 5 + var, :], start=(i == 0), stop=(i == len(srcs) - 1))
                    return ins
                P.op("pe", fn, reads=rb + ["BND"], writes=[psb(b)])
            for g in range(4):
                evac_copy(g, DT[g][:, 0:T], PS[:, bb[g], 0:T], [psb(bb[g])], [("DT", g)])
            b2 = [nbank() for _ in range(4)]
            for g in range(4):
                mm(PS[:, b2[g], 0:T], [(WPG[:, g, :], DT[g][:, 0:T])], reads=["WPG", ("DT", g)], writes=[psb(b2[g])])
            for g in range(4):
                P.op("dve", lambda e, g=g, T=T, b=b2[g]: e.tensor_scalar(PST[g][:, 0:T], PS[:, b, 0:T], PSC[:, g:g + 1], None, ALU.mult),
                     reads=[psb(b2[g]), "PSC"], writes=[("PST", g)])
                dma_sp("ps%d" % g, pool_d[g, :, t0:t0 + T], PST[g][:, 0:T], reads=[("PST", g)], writes=[("Pd", g, t0 // 512)])

    def emit_attn(l, last):
        A.reset()
        COS = A.alloc([128, S], BF16)
        SIN = A.alloc([128, S], BF16)
        PROT = A.alloc([128, 128], BF16)
        WV = A.alloc([128, 8, 512], BF16)
        VH = A.alloc([128, NB, 512], BF16)
        WK = A.alloc([128, 8, 128], BF16)
        WQ = A.alloc([128, 8, 128], BF16)
        KT = A.alloc([128, NT], BF16)
        QT = A.alloc([128, NT], BF16)
        KB_ = [A.alloc([128, 512], BF16) for _ in range(2)]
        PT = [A.alloc([128, 2, 512], BF16) for _ in range(3)]
        TT = A.alloc([128, 4, 512], F32)
        SQ = A.alloc([128, 512], BF16)
        LNS = A.alloc([128, 512], F32)
        ATS = [A.alloc([128, 512], BF16) for _ in range(2)]
        dma_sp("ldc", COS, cos_d[:, :], writes=["COS"])
        dma_sp("ldc", SIN, sin_d[:, :], writes=["SIN"])
        dma_sp("ldc", PROT, prot_d[:, :], writes=["PROT"])
        ev = 0
        ropei = 0
        ats_i = 0
        for h in range(H):
            hh = h % 4
            bank_ctr[0] = 0
            if hh == 0:
                dma_cast("wv", WV, wview(w_in[l], OFF_V + (h // 4) * 512, 512), writes=["WV"])
                for tb in range(NB):
                    b = nbank()
                    mm(PS[:, b, :], [(UT[:, kc, tb * 128:(tb + 1) * 128], WV[:, kc, :]) for kc in range(8)],
                       reads=ut_keys(tb * 128, 128) + ["WV"], writes=[psb(b)])
                    evac_copy(ev, VH[:, tb, :], PS[:, b, :], [psb(b)], [("VH", tb)])
                    ev += 1
            dma_cast("wk", WK, wview(w_in[l], OFF_K + h * 128, 128), writes=["WK"])
            dma_cast("wq", WQ, wview(w_in[l], OFF_Q + h * 128, 128), writes=["WQ"])
            rope_pending = [None]
            for which, Wt, dst in (("K", WK, KT), ("Q", WQ, QT)):
                for ci, (t0, T) in enumerate(TCH):
                    if which == "Q" and last and t0 >= S:
                        continue
                    b = nbank()
                    mm(PS[:, b, 0:T], [(Wt[:, kc, :], UT[:, kc, t0:t0 + T]) for kc in range(8)],
                       reads=["W" + which] + ut_keys(t0, T), writes=[psb(b)])
                    if t0 < S:
                        s_ = ropei % 2
                        ropei += 1
                        ta, tb_ = TT[:, 2 * s_, :], TT[:, 2 * s_ + 1, :]
                        P.op("act", lambda e, s_=s_, b=b: e.activation(out=KB_[s_], in_=PS[:, b, :], func=AF.Copy),
                             reads=[psb(b)], writes=[("KB", s_)])
                        P.op("dve", lambda e, ta=ta, b=b, t0=t0: e.tensor_tensor(ta, PS[:, b, :], COS[:, t0:t0 + 512], ALU.mult),
                             reads=[psb(b), "COS"], writes=[("TT", 2 * s_)])
                        def rope_fin(s_=s_, ta=ta, tb_=tb_, dst=dst, t0=t0, which=which, ci=ci):
                            b2 = nbank()
                            mm(PS[:, b2, :], [(PROT, KB_[s_])], reads=["PROT", ("KB", s_)], writes=[psb(b2)])
                            P.op("dve", lambda e: e.tensor_tensor(tb_, PS[:, b2, :], SIN[:, t0:t0 + 512], ALU.mult),
                                 reads=[psb(b2), "SIN"], writes=[("TT", 2 * s_ + 1)])
                            P.op("dve", lambda e: e.tensor_tensor(dst[:, t0:t0 + 512], ta, tb_, ALU.add),
                                 reads=[("TT", 2 * s_), ("TT", 2 * s_ + 1)], writes=[(which + "T", ci)])
                        if rope_pending[0] is not None:
                            rope_pending[0]()
                        rope_pending[0] = rope_fin
                    else:
                        evac_copy(ev, dst[:, t0:t0 + T], PS[:, b, 0:T], [psb(b)], [(which + "T", ci)])
                        ev += 1
            if rope_pending[0] is not None:
                rope_pending[0]()
                rope_pending[0] = None
            qsets = [((q0, 512), list(range(NB)), qi) for qi, q0 in enumerate((0, 512, 1024, 1536))]
            if not last:
                qsets.append(((S, 256), [16, 17], 4))
            it = 0
            pti = 0
            pending = [None]
            acc_i = [0]
            ACC = [LNBC[:, 0, 0:512], LNBC[:, 1, 0:512]]

            def flush():
                if pending[0] is not None:
                    pending[0]()
                    pending[0] = None

            def make_em(q0, Tq, kbs, qi):
                slots = {}

                def emit_s(i, kbs=kbs, q0=q0, Tq=Tq, qi=qi, slots=slots):
                    nonlocal it, pti
                    kb = kbs[i]
                    sp_ = (0, 1) if it % 2 == 0 else (2, 3)
                    it += 1
                    j = pti % 3
                    pti += 1
                    slots[i] = j

                    def sfn(e):
                        e.matmul(PS[:, sp_[0], 0:Tq], KT[0:64, kb * 128:(kb + 1) * 128], QT[0:64, q0:q0 + Tq], start=True, stop=True)
                        return e.matmul(PS[:, sp_[1], 0:Tq], KT[64:128, kb * 128:(kb + 1) * 128], QT[64:128, q0:q0 + Tq], start=True, stop=True)
                    P.op("pe", sfn, reads=[("KT", kb // 4), ("QT", qi)], writes=[psb(sp_[0]), psb(sp_[1])])
                    P.op("act", lambda e: e.activation(out=PT[j][:, :, 0:Tq], in_=PS[:, sp_[0]:sp_[0] + 2, 0:Tq],
                                                       func=AF.Exp, scale=0.125),
                         reads=[psb(sp_[0]), psb(sp_[1])], writes=[("PT", j)])

                def emit_av(i, kbs=kbs, Tq=Tq, slots=slots):
                    kb = kbs[i]
                    j = slots[i]
                    first, lastk = (i == 0), (i == len(kbs) - 1)
                    hh_ = hh

                    def afn(e):
                        v = VH[:, kb, hh_ * 128:(hh_ + 1) * 128]
                        e.matmul(PS[:, 4, 0:Tq], v, PT[j][:, 0, 0:Tq], start=first, stop=lastk)
                        e.matmul(PS[:, 5, 0:Tq], v, PT[j][:, 1, 0:Tq], start=first, stop=lastk)
                        return e.matmul(PS[:, 7, 0:Tq], ONES[:], PT[j][:, 1, 0:Tq], start=first, stop=lastk)
                    P.op("pe", afn, reads=[("VH", kb), ("PT", j), "ONES"], writes=[psb(4), psb(5), psb(7)])
                    acc = ACC[acc_i[0] % 2]
                    if first:
                        P.op("dve", lambda e, acc=acc: e.tensor_copy(acc[:, 0:Tq], PT[j][:, 0, 0:Tq]),
                             reads=[("PT", j)], writes=[("ACC", acc_i[0] % 2)])
                    else:
                        P.op("dve", lambda e, acc=acc: e.tensor_tensor(acc[:, 0:Tq], acc[:, 0:Tq], PT[j][:, 0, 0:Tq], ALU.add),
                             reads=[("PT", j), ("ACC", acc_i[0] % 2)], writes=[("ACC", acc_i[0] % 2)])
                    if lastk:
                        mm(PS[:, 6, 0:Tq], [(ONES32[:], acc[:, 0:Tq])], reads=["ONES32", ("ACC", acc_i[0] % 2)], writes=[psb(6)])
                        acc_i[0] += 1

                return emit_s, emit_av

            ems = [make_em(q0_, Tq_, kbs_, qi_) for (q0_, Tq_), kbs_, qi_ in qsets]
            ems[0][0](0)
            for qn, ((q0, Tq), kbs, qi) in enumerate(qsets):
                emit_s, emit_av = ems[qn]
                for i in range(len(kbs)):
                    if i + 1 < len(kbs):
                        emit_s(i + 1)
                    elif qn + 1 < len(qsets):
                        ems[qn + 1][0](0)
                    emit_av(i)
                    if i == min(5, len(kbs) - 1):
                        flush()
                ta, tb_, tc, td = TT[:, 0, 0:Tq], TT[:, 1, 0:Tq], TT[:, 2, 0:Tq], TT[:, 3, 0:Tq]
                P.op("dve", lambda e, tc=tc, Tq=Tq: e.tensor_copy(tc, PS[:, 5, 0:Tq]), reads=[psb(5)], writes=[("TT", 2)])
                P.op("dve", lambda e, td=td, Tq=Tq: e.tensor_copy(td, PS[:, 4, 0:Tq]), reads=[psb(4)], writes=[("TT", 3)])
                for bnk, tx, ix in ((7, tb_, 1), (6, ta, 0)):
                    P.op("act", lambda e, bnk=bnk, tx=tx, Tq=Tq: e.activation(out=tx, in_=PS[:, bnk, 0:Tq], func=AF.Ln),
                         reads=[psb(bnk)], writes=[("TT", ix)])
                for tx, ix in ((tb_, 1), (ta, 0)):
                    P.op("act", lambda e, tx=tx: e.activation(out=tx, in_=tx, func=AF.Exp, scale=-1.0),
                         reads=[("TT", ix)], writes=[("TT", ix)])
                P.op("dve", lambda e, tc=tc, tb_=tb_: e.tensor_tensor(tc, tc, tb_, ALU.mult),
                     reads=[("TT", 2), ("TT", 1)], writes=[("TT", 2)])
                P.op("dve", lambda e, td=td, ta=ta: e.tensor_tensor(td, td, ta, ALU.mult),
                     reads=[("TT", 3), ("TT", 0)], writes=[("TT", 3)])
                P.op("dve", lambda e, td=td, tc=tc: e.scalar_tensor_tensor(td, tc, NLAM[:, 0:1], td, ALU.mult, ALU.add),
                     reads=[("TT", 2), ("TT", 3), "NLAM"], writes=[("TT", 3)])

                def tail(td=td, Tq=Tq, q0=q0, qi=qi, h=h):
                    nonlocal it, ats_i
                    sb_ = 0 if it % 2 == 0 else 2
                    it += 1
                    a_ = ats_i % 2
                    ats_i += 1
                    P.op("act", lambda e: e.activation(out=SQ[:, 0:Tq], in_=td, func=AF.Square), reads=[("TT", 3)], writes=["SQ"])
                    mm(PS[:, sb_, 0:Tq], [(ONES[:], SQ[:, 0:Tq])], reads=["ONES", "SQ"], writes=[psb(sb_), psb(sb_ + 1)])
                    P.op("act", lambda e: e.activation(out=LNS[:, 0:Tq], in_=PS[:, sb_, 0:Tq], func=AF.Ln, bias=128.0 * EPS, scale=1.0),
                         reads=[psb(sb_)], writes=["LNS"])
                    P.op("act", lambda e: e.activation(out=LNS[:, 0:Tq], in_=LNS[:, 0:Tq], func=AF.Exp, scale=-0.5),
                         reads=["LNS"], writes=["LNS"])
                    P.op("dve", lambda e: e.scalar_tensor_tensor(ATS[a_][:, 0:Tq], td, SGS[:, 0:1], LNS[:, 0:Tq], ALU.mult, ALU.mult),
                         reads=[("TT", 3), "LNS", "SGS"], writes=[("ATS", a_)])
                    dma_sp("ats%d" % a_, att_d[h, :, q0:q0 + Tq], ATS[a_][:, 0:Tq], reads=[("ATS", a_)], writes=[("ATTd", h, qi)])
                pending[0] = tail
            flush()

    def epi_a(tb, banks, gsel, TMP, tmpkeys):
        for ch in range(2):
            P.op("dve", lambda e, ch=ch: e.tensor_tensor(TMP[:, ch * 512:(ch + 1) * 512], PS[:, banks[ch], :],
                                                         GBC[:, gsel, ch * 512:(ch + 1) * 512], ALU.mult),
                 reads=[psb(banks[ch]), ("GBC", gsel, ch)], writes=tmpkeys)
        P.op("dve", lambda e: e.scalar_tensor_tensor(X[:, tb, :], X[:, tb, :], ALPHA, TMP, ALU.mult, ALU.add),
             reads=[("X", tb)] + tmpkeys, writes=[("X", tb)])

    def epi_b(tb, TMP, tmpkeys):
        P.op("dve", lambda e: [e.bn_stats(ST[:, tb, 0, :], X[:, tb, 0:512]), e.bn_stats(ST[:, tb, 1, :], X[:, tb, 512:1024])],
             reads=[("X", tb)], writes=[("ST", tb)])
        P.op("dve", lambda e: e.bn_aggr(MV[:, tb, :], ST[:, tb, :, :]), reads=[("ST", tb)], writes=[("MV", tb)])
        P.op("act", lambda e: e.activation(out=LNV[:, tb:tb + 1], in_=MV[:, tb, 1:2], func=AF.Ln, bias=EPS, scale=1.0),
             reads=[("MV", tb)], writes=[("LNV", tb)])
        P.op("act", lambda e: e.activation(out=RS[:, tb:tb + 1], in_=LNV[:, tb:tb + 1], func=AF.Exp, scale=-0.5),
             reads=[("LNV", tb)], writes=[("RS", tb)])
        P.op("dve", lambda e: e.scalar_tensor_tensor(NBI[:, tb:tb + 1], MV[:, tb, 0:1], -1.0, RS[:, tb:tb + 1], ALU.mult, ALU.mult),
             reads=[("MV", tb), ("RS", tb)], writes=[("NBI", tb)])
        P.op("act", lambda e: e.activation(out=TMP, in_=X[:, tb, :], func=AF.Identity, scale=RS[:, tb:tb + 1], bias=NBI[:, tb:tb + 1]),
             reads=[("X", tb), ("RS", tb), ("NBI", tb)], writes=tmpkeys)
        P.op("dve", lambda e: e.tensor_tensor(TMP, TMP, LNBC[:, 0, :], ALU.mult), reads=tmpkeys + ["LNBC"], writes=tmpkeys)
        P.op("dve", lambda e: e.tensor_tensor(X[:, tb, :], TMP, LNBC[:, 1, :], ALU.add), reads=tmpkeys + ["LNBC"], writes=[("X", tb)])

    def epilogue(tb, banks, gsel, TMP, tmpkeys):
        epi_a(tb, banks, gsel, TMP, tmpkeys)
        epi_b(tb, TMP, tmpkeys)

    def emit_merge(l, last):
        A.reset()
        chunks = TCH[:4] if last else TCH
        WO = A.alloc([128, 8, D], BF16)
        ATTC = A.alloc([128, 8, 512], BF16)
        YC = A.alloc([128, 4, 512], BF16)
        PC = A.alloc([128, 4, 512], BF16)
        WG3 = [A.alloc([128, 8, 3, 128], BF16) for _ in range(2)]
        WAB2 = [A.alloc([128, 8, 128], BF16) for _ in range(2)]
        WFB2 = [A.alloc([128, 4, 128], BF16) for _ in range(2)]
        WPB2 = [A.alloc([128, 4, 128], BF16) for _ in range(2)]
        SGt = [A.alloc([128, 512], F32) for _ in range(3)]
        TT = A.alloc([128, 2, 512], F32)
        MT = A.alloc([128, 8, 512], BF16)
        TMP = TT[:, 0:2, :].rearrange("p a n -> p (a n)")
        dma_cast("wo", WO, wview(w_out[l], 0, D), writes=["WO"])
        dma_sp("lnbc", LNBC[:, 0, :], ln1_g[l].partition_broadcast(128), writes=["LNBC"])
        dma_sp("lnbc", LNBC[:, 1, :], ln1_b[l].partition_broadcast(128), writes=["LNBC"])
        wgv = w_in[l][:, OFF_G:OFF_G + 3 * D].rearrange("(kc p) (i n) -> p kc i n", p=128, i=3)
        it = 0
        for ci, (t0, T) in enumerate(chunks):
            dma_sp("ldatt", ATTC[:, :, 0:T], att_d[:, :, t0:t0 + T].rearrange("h p t -> p h t"),
                   reads=[("ATTd", h, ci) for h in range(H)], writes=["ATTC"])
            dma_sp("ldy", YC[:, :, 0:T], y_d[:, :, t0:t0 + T].rearrange("g p t -> p g t"),
                   reads=[("Yd", g, ci) for g in range(4)], writes=["YC"])
            dma_sp("ldp", PC[:, :, 0:T], pool_d[:, :, t0:t0 + T].rearrange("g p t -> p g t"),
                   reads=[("Pd", g, ci) for g in range(4)], writes=["PC"])
            for fc in range(8):
                slot = it % 2
                it += 1
                WAB, WFB, WPB = WAB2[slot], WFB2[slot], WPB2[slot]
                g3flat = WG3[slot].rearrange("p a b c -> p (a b c)")
                abflat = WAB.rearrange("p a b -> p (a b)")
                fbflat = WFB.rearrange("p a b -> p (a b)")
                pbflat = WPB.rearrange("p a b -> p (a b)")
                if ci == 0:
                    P.op("pool", lambda e, slot=slot, fc=fc: [e.dma_start(out=WG3[slot][:, :, i, :], in_=wgv[:, :, i, fc * 128:(fc + 1) * 128])
                                                               for i in range(3)], writes=[("WG3", slot)], dma="wg%d" % slot, ndma=3)
                    dma_cast("wab%d" % slot, WAB, w_att_br[l][:, fc * 128:(fc + 1) * 128].rearrange("(h p) n -> p h n", p=128), writes=[("WAB", slot)])
                    dma_cast("wfb%d" % slot, WFB, w_four_br[l][:, fc * 128:(fc + 1) * 128].rearrange("(g p) n -> p g n", p=128), writes=[("WFB", slot)])
                    dma_cast("wpb%d" % slot, WPB, w_pool_br[l][:, fc * 128:(fc + 1) * 128].rearrange("(g p) n -> p g n", p=128), writes=[("WPB", slot)])
                    dma_sp("wg3s%d" % slot, wg3_scr[fc], g3flat, reads=[("WG3", slot)], writes=[("WG3d", fc)])
                    dma_sp("wabs%d" % slot, wab_scr[fc], abflat, reads=[("WAB", slot)], writes=[("WABd", fc)])
                    dma_sp("wfbs%d" % slot, wfb_scr[fc], fbflat, reads=[("WFB", slot)], writes=[("WFBd", fc)])
                    dma_sp("wpbs%d" % slot, wpb_scr[fc], pbflat, reads=[("WPB", slot)], writes=[("WPBd", fc)])
                else:
                    dma_sp("wg3l%d" % slot, g3flat, wg3_scr[fc], reads=[("WG3d", fc)], writes=[("WG3", slot)])
                    dma_sp("wabl%d" % slot, abflat, wab_scr[fc], reads=[("WABd", fc)], writes=[("WAB", slot)])
                    dma_sp("wfbl%d" % slot, fbflat, wfb_scr[fc], reads=[("WFBd", fc)], writes=[("WFB", slot)])
                    dma_sp("wpbl%d" % slot, pbflat, wpb_scr[fc], reads=[("WPBd", fc)], writes=[("WPB", slot)])
                bG = [nbank() for _ in range(3)]
                bA, bF, bP = nbank(), nbank(), nbank()
                for i in range(3):
                    mm(PS[:, bG[i], 0:T], [(WG3[slot][:, kc, i, :], UT[:, kc, t0:t0 + T]) for kc in range(8)],
                       reads=[("WG3", slot)] + ut_keys(t0, T), writes=[psb(bG[i])])
                    P.op("act", lambda e, i=i, b=bG[i], T=T: e.activation(out=SGt[i][:, 0:T], in_=PS[:, b, 0:T], func=AF.Sigmoid),
                         reads=[psb(bG[i])], writes=[("SGt", i)])
                mm(PS[:, bA, 0:T], [(WAB[:, h, :], ATTC[:, h, 0:T]) for h in range(H)], reads=[("WAB", slot), "ATTC"], writes=[psb(bA)])
                mm(PS[:, bF, 0:T], [(WFB[:, g, :], YC[:, g, 0:T]) for g in range(4)], reads=[("WFB", slot), "YC"], writes=[psb(bF)])
                mm(PS[:, bP, 0:T], [(WPB[:, g, :], PC[:, g, 0:T]) for g in range(4)], reads=[("WPB", slot), "PC"], writes=[psb(bP)])
                P.op("dve", lambda e, T=T, b=bA: e.tensor_tensor(TT[:, 0, 0:T], PS[:, b, 0:T], SGt[0][:, 0:T], ALU.mult),
                     reads=[psb(bA), ("SGt", 0)], writes=[("MTT", 0)])
                P.op("dve", lambda e, T=T, b=bF: e.tensor_tensor(TT[:, 1, 0:T], PS[:, b, 0:T], SGt[1][:, 0:T], ALU.mult),
                     reads=[psb(bF), ("SGt", 1)], writes=[("MTT", 1)])
                P.op("dve", lambda e, T=T: e.tensor_tensor(TT[:, 0, 0:T], TT[:, 0, 0:T], TT[:, 1, 0:T], ALU.add),
                     reads=[("MTT", 0), ("MTT", 1)], writes=[("MTT", 0)])
                P.op("dve", lambda e, T=T, b=bP: e.tensor_tensor(TT[:, 1, 0:T], PS[:, b, 0:T], SGt[2][:, 0:T], ALU.mult),
                     reads=[psb(bP), ("SGt", 2)], writes=[("MTT", 1)])
                P.op("dve", lambda e, T=T, fc=fc: e.tensor_tensor(MT[:, fc, 0:T], TT[:, 0, 0:T], TT[:, 1, 0:T], ALU.add),
                     reads=[("MTT", 0), ("MTT", 1)], writes=[("MT", fc)])
            for tbl in range(T // 128):
                tb = t0 // 128 + tbl
                banks = (nbank(), nbank())
                for ch in range(2):
                    mm(PS[:, banks[ch], :], [(MT[:, fc, tbl * 128:(tbl + 1) * 128], WO[:, fc, ch * 512:(ch + 1) * 512]) for fc in range(8)],
                       reads=[("MT", fc) for fc in range(8)] + ["WO"], writes=[psb(banks[ch])])
                epi_a(tb, banks, 0 if t0 < S else 2, TMP, [("MTT", 0), ("MTT", 1)])
            for tbl in range(T // 128):
                epi_b(t0 // 128 + tbl, TMP, [("MTT", 0), ("MTT", 1)])

    def emit_ffn(l, last):
        A.reset()
        passes = [(0, 1024), (1024, 1024)] + ([] if last else [(2048, 256)])
        HT = A.alloc([128, NFF, 1024], BF16)
        WGU = [A.alloc([128, 8, 2, 256], BF16) for _ in range(2)]
        WD = [A.alloc([128, 2, D], BF16),
              GBC[:, 0, :].bitcast(BF16).rearrange("p (j n) -> p j n", j=2),
              GBC[:, 2, :].bitcast(BF16).rearrange("p (j n) -> p j n", j=2)]
        SL = [A.alloc([128, 512], F32)]
        TMP = A.alloc([128, D], F32)
        dma_sp("lnbc", LNBC[:, 0, :], ln2_g[l].partition_broadcast(128), writes=["LNBC"])
        dma_sp("lnbc", LNBC[:, 1, :], ln2_b[l].partition_broadcast(128), writes=["LNBC"])
        gi = 0
        di = 0
        for (p0, TP) in passes:
            halves = [(h0, min(512, TP - h0)) for h0 in range(0, TP, 512)]
            for ffg in range(NFF // 2):
                slot = gi % 2
                gi += 1
                wflat = WGU[slot].rearrange("p a b c -> p (a b c)")
                if p0 == 0:
                    dma_cast("wgu%d" % slot, WGU[slot][:, :, 0, :], wview(w_ffn_gate[l], ffg * 256, 256), writes=[("WGU", slot, 0)])
                    dma_cast("wgu%d" % slot, WGU[slot][:, :, 1, :], wview(w_ffn_up[l], ffg * 256, 256), writes=[("WGU", slot, 1)])
                    dma_sp("wgus%d" % slot, wgu_scr[ffg], wflat, reads=[("WGU", slot, 0), ("WGU", slot, 1)], writes=[("WGUd", ffg)])
                else:
                    dma_sp("wgul%d" % slot, wflat, wgu_scr[ffg], reads=[("WGUd", ffg)], writes=[("WGU", slot, 0), ("WGU", slot, 1)])
                for j in range(2):
                    ff = ffg * 2 + j
                    for (h0, T) in halves:
                        t0 = p0 + h0
                        bg, bu = nbank(), nbank()
                        mm(PS[:, bg, 0:T], [(WGU[slot][:, kc, 0, j * 128:(j + 1) * 128], UT[:, kc, t0:t0 + T]) for kc in range(8)],
                           reads=[("WGU", slot, 0)] + ut_keys(t0, T), writes=[psb(bg)])
                        mm(PS[:, bu, 0:T], [(WGU[slot][:, kc, 1, j * 128:(j + 1) * 128], UT[:, kc, t0:t0 + T]) for kc in range(8)],
                           reads=[("WGU", slot, 1)] + ut_keys(t0, T), writes=[psb(bu)])
                        P.op("act", lambda e, bg=bg, T=T: e.activation(out=SL[0][:, 0:T], in_=PS[:, bg, 0:T], func=AF.Silu),
                             reads=[psb(bg)], writes=[("SL", 0)])
                        P.op("dve", lambda e, bu=bu, T=T, ff=ff, h0=h0: e.tensor_tensor(HT[:, ff, h0:h0 + T], SL[0][:, 0:T], PS[:, bu, 0:T], ALU.mult),
                             reads=[psb(bu), ("SL", 0)], writes=[("HT", ff, h0 // 512)])
            for r0 in range(0, TP // 128, 4):
                nb_ = min(4, TP // 128 - r0)
                banks = [(2 * t, 2 * t + 1) for t in range(nb_)]
                allb = [psb(b) for pr in banks for b in pr]
                for ffg in range(NFF // 2):
                    slot = di % 3
                    di += 1
                    wdflat = WD[slot].rearrange("p j n -> p (j n)")
                    if p0 == 0 and r0 == 0:
                        dma_cast("wd%d" % slot, WD[slot], w_ffn_down[l][ffg * 256:(ffg + 1) * 256, :].rearrange("(j p) n -> p j n", p=128),
                                 writes=[("WD", slot)])
                        dma_sp("wds%d" % slot, wd_scr[ffg], wdflat, reads=[("WD", slot)], writes=[("WDd", ffg)])
                    elif ffg % 2 == 0:
                        dma_sp("wdl%d" % slot, wdflat, wd_scr[ffg], reads=[("WDd", ffg)], writes=[("WD", slot)])
                    else:
                        P.op("pool", lambda e, wdflat=wdflat, ffg=ffg: e.dma_start(out=wdflat, in_=wd_scr[ffg]),
                             reads=[("WDd", ffg)], writes=[("WD", slot)], dma="wdp%d" % slot)

                    def fn(e, ffg=ffg, slot=slot, banks=banks, nb_=nb_, r0=r0):
                        ins = None
                        for j in range(2):
                            ff = ffg * 2 + j
                            for t in range(nb_):
                                for ch in range(2):
                                    c0 = (r0 + t) * 128
                                    ins = e.matmul(PS[:, banks[t][ch], :], HT[:, ff, c0:c0 + 128], WD[slot][:, j, ch * 512:(ch + 1) * 512],
                                                   start=(ff == 0), stop=(ff == NFF - 1))
                        return ins
                    P.op("pe", fn, reads=[("HT", ffg * 2, r0 // 4), ("HT", ffg * 2 + 1, r0 // 4), ("WD", slot)], writes=allb)
                for t in range(nb_):
                    epi_a(p0 // 128 + r0 + t, banks[t], 1 if p0 < S else 3, TMP, ["FTMP"])
                for t in range(nb_):
                    epi_b(p0 // 128 + r0 + t, TMP, ["FTMP"])
            bank_ctr[0] = 0

    def tap(name, src_ap, shape, dt, reads):
        if name in dbg:
            t = dr_now("dbg_" + name, shape, dt, "ExternalOutput")
            dbg_out[name] = t
            dma_sp("dbg", t, src_ap, reads=reads, writes=[("dbg", name)])

    order = ["mod", "ln1", "four", "pool", "attn", "merge", "ln2", "ffn"]
    upto = dbg.get("upto", "ffn")
    nph = order.index(upto) + 1
    for l in range(n_layers):
        last = l == DEPTH - 1
        steps = [
            (lambda: None) if dbg.get("skip_mod") else (lambda: emit_mod(l)),
            lambda: emit_ln_ut(0, 1, NB, reset=bool(dbg.get("skip_mod"))),
            lambda: emit_fourier(l, last),
            lambda: emit_pool(l, last),
            lambda: emit_attn(l, last),
            lambda: emit_merge(l, last),
            lambda: emit_ln_ut(2, 3, 16 if last else NB),
            lambda: emit_ffn(l, last),
        ]
        for si, st in enumerate(steps[:nph]):
            st()
            if si != 0:
                P.fence()
            if l == 0 and si == 1:
                tap("UT", UT[:], [128, 8, NT], BF16, [])
                tap("MODT", MODT[:], [128, 4, 8, 2], F32, [])
                tap("GBC", GBC[:], [128, 4, D], F32, [])
                tap("NLAM", NLAM[:], [128, 1], F32, [])
            if l == 0 and si == 5:
                tap("xmix", X[:], [128, NB, D], F32, [])
    if "xall" in dbg:
        tap("xall", X[:], [128, NB, D], F32, [])
    ov = out_d.rearrange("(tb p) d -> p tb d", p=128)
    for i in range(4):
        dma_sp("st%d" % i, ov[:, 4 * i:4 * i + 4, :], X[:, 4 * i:4 * i + 4, :], reads=[("X", t) for t in range(4 * i, 4 * i + 4)],
               writes=[("out", i)])
    P.fence()
    stats = P.emit(nc, es)
    es.close()
    nc._declared_inputs = set(declared)
    return nc, stats, (dbg_out, dict(att=att_d, y=y_d, pool=pool_d))


_CONSTS = None
_NC_CACHE = {}


def make_in_maps(inputs):
    global _CONSTS
    if _CONSTS is None:
        _CONSTS = make_consts()
    f = lambda a: np.ascontiguousarray(np.asarray(a, np.float32))
    shared = {
        "w_mod": f(inputs["w_mod"]), "b_mod": f(inputs["b_mod"]),
        "b_modT": f(np.asarray(inputs["b_mod"], np.float32).reshape(DEPTH, 48, 128).transpose(0, 2, 1)),
        "w_in": f(inputs["w_in"]), "lam_qk": f(np.asarray(inputs["lam_qk"], np.float32).reshape(DEPTH, 256)),
        "subln_gT": f(np.asarray(inputs["subln_g"], np.float32).reshape(DEPTH, 128, 1)),
        "w_att_br": f(inputs["w_att_br"]), "w_four_br": f(inputs["w_four_br"]), "w_pool_grp": f(inputs["w_pool_grp"]),
        "pool_scaleT": f(np.asarray(inputs["pool_scale"], np.float32).reshape(DEPTH, 4, 128).transpose(0, 2, 1)),
        "w_pool_br": f(inputs["w_pool_br"]), "w_out": f(inputs["w_out"]),
        "ln1_g": f(inputs["ln1_g"]), "ln1_b": f(inputs["ln1_b"]),
        "w_ffn_gate": f(inputs["w_ffn_gate"]), "w_ffn_up": f(inputs["w_ffn_up"]), "w_ffn_down": f(inputs["w_ffn_down"]),
        "ln2_g": f(inputs["ln2_g"]), "ln2_b": f(inputs["ln2_b"]),
    }
    shared.update(_CONSTS)
    x = np.asarray(inputs["x"], np.float32)
    c = np.asarray(inputs["c"], np.float32)
    ctx = np.asarray(inputs["ctx"], np.float32)
    cc = np.asarray(inputs["c_ctx"], np.float32)
    maps = []
    for b in range(x.shape[0]):
        m = dict(shared)
        m["x"] = f(x[b])
        m["ctx"] = f(ctx[b])
        m["cT"] = f(np.concatenate([c[b].reshape(8, 128).T, cc.reshape(8, 128).T], axis=1))
        maps.append(m)
    return maps


def kernel(**inputs):
    if "full" not in _NC_CACHE:
        _NC_CACHE["full"] = build_program(DEPTH)[0]
    nc = _NC_CACHE["full"]
    maps = [{k: v for k, v in m.items() if k in nc._declared_inputs} for m in make_in_maps(inputs)]
    res = run_bass_kernel_spmd(nc, maps, core_ids=list(range(len(maps))))
    out = np.stack([np.asarray(r["out"], np.float32).reshape(S, D) for r in res.results], axis=0)
    return out
```
